# Optimizing a Trainium2 kernel written in Bass

```python
import jax, jax.numpy as jnp
from jax import lax
import numpy as np

D_MODEL = 1024
BATCH = 16
SEQ = 4096
DEPTH = 1
DEC_BATCH = 2
DEC_SEQ = 8192
PAST_LEN = 128

MIX_WIDTH = D_MODEL
POOL_WIDTH = MIX_WIDTH // 2
ATTN_WIDTH = MIX_WIDTH - POOL_WIDTH
POOL_WINDOWS = (2, 4, 8, 16)
N_POOL_GROUPS = len(POOL_WINDOWS)
POOL_GROUP_DIM = POOL_WIDTH // N_POOL_GROUPS
HEAD_DIM = 64
N_HEADS = ATTN_WIDTH // HEAD_DIM
GRID_W = 64
WIN_ROWS_MAX = 8
WIN_COLS = 16
RPB_ROWS = 2 * WIN_ROWS_MAX - 1
RPB_COLS = 2 * WIN_COLS - 1
IN_WIDTH = 2 * POOL_WIDTH + 4 * ATTN_WIDTH
EPS = 1e-6

kernel_name = "hymba_pool_natten_encoder"


def rms_norm(x, g):
    xf = x.astype(jnp.float32)
    y = xf * lax.rsqrt(jnp.mean(xf * xf, axis=-1, keepdims=True) + EPS)
    return (y * g.astype(jnp.float32)).astype(x.dtype)


def centred_window_mean(u, w):
    B, S, C = u.shape
    a = w // 2
    b = w - a
    csum = jnp.cumsum(u.astype(jnp.float32), axis=1)
    csum = jnp.concatenate([jnp.zeros((B, 1, C), jnp.float32), csum], axis=1)
    cpad = jnp.pad(csum, ((0, 0), (a, b), (0, 0)), mode="edge")
    window_sum = cpad[:, w:w + S] - cpad[:, 0:S]
    t = jnp.arange(S)
    count = (jnp.minimum(t + b, S) - jnp.maximum(t - a, 0)).astype(jnp.float32)
    return window_sum / count[None, :, None]


def pool_mixer(u, w_pool, pool_scale):
    B, S, _ = u.shape
    ug = u.reshape(B, S, N_POOL_GROUPS, POOL_GROUP_DIM)
    pooled = jnp.stack(
        [centred_window_mean(ug[:, :, g], w) - ug[:, :, g].astype(jnp.float32)
         for g, w in enumerate(POOL_WINDOWS)], axis=2)
    mixed = jnp.einsum("bsgc,gcd->bsgd", pooled.astype(u.dtype), w_pool)
    return mixed.reshape(B, S, POOL_WIDTH) * pool_scale


def neighbourhood_attention(q, k, v, rpb):
    B, S, H, Dh = q.shape
    rows = S // GRID_W
    kr = min(WIN_ROWS_MAX, rows)
    qg = q.reshape(B, rows, GRID_W, H, Dh)
    kg = k.reshape(B, rows, GRID_W, H, Dh)
    vg = v.reshape(B, rows, GRID_W, H, Dh)
    col = np.arange(GRID_W)
    col_start = np.clip(col - WIN_COLS // 2, 0, GRID_W - WIN_COLS)
    col_idx = col_start[:, None] + np.arange(WIN_COLS)[None, :]
    col_off = col_idx - col[:, None] + (WIN_COLS - 1)
    bias_cols = rpb[:, :, col_off]

    def row_block(r):
        rs = jnp.clip(r - kr // 2, 0, rows - kr)
        q_r = lax.dynamic_index_in_dim(qg, r, axis=1, keepdims=False)
        k_r = lax.dynamic_slice_in_dim(kg, rs, kr, axis=1)
        v_r = lax.dynamic_slice_in_dim(vg, rs, kr, axis=1)
        k_w = k_r[:, :, col_idx]
        v_w = v_r[:, :, col_idx]
        s = jnp.einsum("bchd,bicjhd->bhcij", q_r, k_w).astype(jnp.float32)
        row_off = rs + jnp.arange(kr) - r + (WIN_ROWS_MAX - 1)
        bias = jnp.take(bias_cols, row_off, axis=1)
        s = s + jnp.transpose(bias, (0, 2, 1, 3)).astype(jnp.float32)[None]
        p = jax.nn.softmax(s.reshape(B, H, GRID_W, kr * WIN_COLS), axis=-1)
        p = p.reshape(B, H, GRID_W, kr, WIN_COLS).astype(v.dtype)
        return jnp.einsum("bhcij,bicjhd->bchd", p, v_w)

    out = lax.map(row_block, jnp.arange(rows))
    return jnp.transpose(out, (1, 0, 2, 3, 4)).reshape(B, S, H * Dh)


def encoder_layer(x, norm_g, w_in, w_pool, pool_scale, q_norm_g, k_norm_g, rpb, w_out):
    B, S, _ = x.shape
    h = rms_norm(x, norm_g)
    proj = h @ w_in
    P, A = POOL_WIDTH, ATTN_WIDTH
    u_pool, g_pool, q, k, v, g_attn = jnp.split(
        proj, [P, 2 * P, 2 * P + A, 2 * P + 2 * A, 2 * P + 3 * A], axis=-1)
    pool_out = pool_mixer(u_pool, w_pool, pool_scale) * jax.nn.silu(g_pool)
    q = rms_norm(q.reshape(B, S, N_HEADS, HEAD_DIM), q_norm_g) * (HEAD_DIM ** -0.5)
    k = rms_norm(k.reshape(B, S, N_HEADS, HEAD_DIM), k_norm_g)
    v = v.reshape(B, S, N_HEADS, HEAD_DIM)
    attn_out = neighbourhood_attention(q, k, v, rpb) * jax.nn.silu(g_attn)
    mixed = jnp.concatenate([pool_out, attn_out], axis=-1)
    return x + mixed @ w_out


def setup_inputs(seed: int = 0) -> dict:
    key = jax.random.key(seed)
    ks = jax.random.split(key, 11)
    f32 = jnp.float32
    x_prompt = jax.random.normal(ks[0], (BATCH, SEQ, D_MODEL), f32)
    x_sample = jax.random.normal(ks[1], (DEC_BATCH, DEC_SEQ, D_MODEL), f32)
    norm_g = 1.0 + 0.02 * jax.random.normal(ks[2], (DEPTH, D_MODEL), f32)
    w_in = jax.random.normal(ks[3], (DEPTH, D_MODEL, IN_WIDTH), f32) * D_MODEL ** -0.5
    w_pool = jax.random.normal(ks[4], (DEPTH, N_POOL_GROUPS, POOL_GROUP_DIM, POOL_GROUP_DIM), f32) * POOL_GROUP_DIM ** -0.5
    pool_scale = 1.0 + 0.02 * jax.random.normal(ks[5], (DEPTH, POOL_WIDTH), f32)
    q_norm_g = 1.0 + 0.02 * jax.random.normal(ks[6], (DEPTH, HEAD_DIM), f32)
    k_norm_g = 1.0 + 0.02 * jax.random.normal(ks[7], (DEPTH, HEAD_DIM), f32)
    rpb = 0.1 * jax.random.normal(ks[8], (DEPTH, N_HEADS, RPB_ROWS, RPB_COLS), f32)
    w_out = jax.random.normal(ks[9], (DEPTH, MIX_WIDTH, D_MODEL), f32) * MIX_WIDTH ** -0.5
    return {"x_prompt": x_prompt, "x_sample": x_sample, "norm_g": norm_g, "w_in": w_in,
            "w_pool": w_pool, "pool_scale": pool_scale, "q_norm_g": q_norm_g,
            "k_norm_g": k_norm_g, "rpb": rpb, "w_out": w_out}


def reference(x_prompt, x_sample, norm_g, w_in, w_pool, pool_scale, q_norm_g, k_norm_g, rpb, w_out):
    y_prompt = x_prompt
    y_sample = x_sample
    for l in range(DEPTH):
        y_prompt = encoder_layer(y_prompt, norm_g[l], w_in[l], w_pool[l], pool_scale[l],
                                 q_norm_g[l], k_norm_g[l], rpb[l], w_out[l])
        y_sample = encoder_layer(y_sample, norm_g[l], w_in[l], w_pool[l], pool_scale[l],
                                 q_norm_g[l], k_norm_g[l], rpb[l], w_out[l])
    return (y_prompt, y_sample)
```

```python
import os
import numpy as np
import ml_dtypes
from contextlib import ExitStack
import concourse.bass as bass
import concourse.mybir as mybir
from concourse.bass_utils import run_bass_kernel_spmd

F32 = mybir.dt.float32
BF16 = mybir.dt.bfloat16
AF = mybir.ActivationFunctionType
ALU = mybir.AluOpType
bf = ml_dtypes.bfloat16

NCORES = 8
D = 1024
EPS = 1e-6
NT_REG, NT_ALL = 8, 14
NTL = NT_REG + NT_ALL
POOL_W = (2, 4, 8, 16)
BK = {"main": 0, "prev": 1, "next": 2, "first": 3, "last": 4, "sfirst": 5, "slast": 6}
NB = 28

COMPUTE = ("pe", "act", "dve", "pool")


class Op:
    __slots__ = ("eng", "fn", "deps", "idx", "need_inc", "inc_no", "dma_key", "dma_cnt", "is_dma")

    def __init__(self, eng, fn):
        self.eng = eng
        self.fn = fn
        self.deps = []
        self.need_inc = False
        self.inc_no = 0
        self.is_dma = False
        self.dma_key = None
        self.dma_cnt = 0


class Prog:
    def __init__(self, same_engine_sync=True):
        self.ops = {e: [] for e in COMPUTE + ("sp", "gq")}
        self.last_w = {}
        self.readers = {}
        self.dma_cnt = {}
        self.same_engine_sync = same_engine_sync
        self.final_dma = []

    def _dep(self, op, other):
        if other is None or other is op:
            return
        if other.eng == op.eng and not other.is_dma:
            if op.eng == "pe" or not self.same_engine_sync:
                return
        if other not in op.deps:
            op.deps.append(other)
            if not other.is_dma:
                other.need_inc = True

    def add(self, eng, fn, reads=(), writes=(), dma_key=None, final=False):
        op = Op(eng, fn)
        if dma_key is not None:
            op.is_dma = True
            op.dma_key = dma_key
            self.dma_cnt[dma_key] = self.dma_cnt.get(dma_key, 0) + 16
            op.dma_cnt = self.dma_cnt[dma_key]
            if final:
                self.final_dma.append(op)
        for r in reads:
            self._dep(op, self.last_w.get(r))
        for w in writes:
            self._dep(op, self.last_w.get(w))
            for rd in self.readers.get(w, ()):
                self._dep(op, rd)
        for r in reads:
            self.readers.setdefault(r, []).append(op)
        for w in writes:
            self.last_w[w] = op
            self.readers[w] = []
        self.ops[eng].append(op)
        return op

    def emit(self, block, sems, dma_sems):
        for e in COMPUTE:
            n = 0
            for op in self.ops[e]:
                if op.need_inc:
                    n += 1
                    op.inc_no = n

        def run(eng_name, h):
            waited = {}
            for op in self.ops[eng_name]:
                for d in op.deps:
                    if d.is_dma:
                        s, v = dma_sems[d.dma_key], d.dma_cnt
                    else:
                        s, v = sems[d.eng], d.inc_no
                    k = id(s)
                    if waited.get(k, 0) < v:
                        h.wait_ge(s, v)
                        waited[k] = v
                ins = op.fn(h)
                if op.is_dma:
                    ins.then_inc(dma_sems[op.dma_key], 16)
                elif op.need_inc:
                    ins.then_inc(sems[op.eng], 1)
            if eng_name == "sp":
                for op in self.final_dma:
                    s, v = dma_sems[op.dma_key], op.dma_cnt
                    if waited.get(id(s), 0) < v:
                        h.wait_ge(s, v)
                        waited[id(s)] = v

        @block.tensor
        def _(h):
            run("pe", h)

        @block.scalar
        def _(h):
            run("act", h)

        @block.vector
        def _(h):
            run("dve", h)

        @block.gpsimd
        def _(h):
            run("pool", h)

        @block.sync
        def _(h):
            run("sp", h)


def tile_specs():
    reg = [(d, d + 1) for d in (2, 1, 0, -1, -2, -3, -4)] + [(3, -4)]
    al = [(d if abs(d) <= 7 else None, d + 1 if abs(d + 1) <= 7 else None) for d in range(6, -8, -1)]
    return reg + al


def build_bias_mask(rpb):
    kc = np.arange(64)[:, None]
    qc = np.arange(64)[None, :]
    cs = np.clip(qc - 8, 0, 48)
    colmask = ((kc >= cs) & (kc < cs + 16)).astype(np.float32)
    dc = np.clip(kc - qc + 15, 0, 30)
    biasg = np.zeros((8, 128, NTL * 64), np.float32)
    maskg = np.zeros((128, NTL * 64), np.float32)
    for ti, drs in enumerate(tile_specs()):
        for kr in range(2):
            dr = drs[kr]
            if dr is None:
                continue
            biasg[:, kr * 64:(kr + 1) * 64, ti * 64:(ti + 1) * 64] = rpb[:, dr + 7][:, dc]
            maskg[kr * 64:(kr + 1) * 64, ti * 64:(ti + 1) * 64] = colmask
    return biasg, maskg


def build_bands(top_edge, bot_edge):
    bands = np.zeros((128, NB, 128), np.float32)
    t = np.arange(128)[:, None]
    tp = np.arange(128)[None, :]
    for g, w in enumerate(POOL_W):
        a = w // 2
        b = w - a
        lo, hi = tp - a, tp + b
        eye = (t == tp).astype(np.float32)
        main = ((t >= lo) & (t < hi)).astype(np.float32) / w - eye
        prev = ((t - 128 >= lo) & (t - 128 < hi)).astype(np.float32) / w
        nxt = ((t + 128 >= lo) & (t + 128 < hi)).astype(np.float32) / w
        cnt_f = (hi - np.maximum(lo, 0)).astype(np.float32)
        first = ((t >= lo) & (t < hi)).astype(np.float32) / cnt_f - eye
        cnt_l = (np.minimum(hi, 128) - lo).astype(np.float32)
        last = ((t >= lo) & (t < hi)).astype(np.float32) / cnt_l - eye
        bands[:, g * 7 + BK["main"]] = main
        bands[:, g * 7 + BK["prev"]] = prev
        bands[:, g * 7 + BK["next"]] = nxt
        bands[:, g * 7 + BK["first"]] = first
        bands[:, g * 7 + BK["last"]] = last
        bands[:, g * 7 + BK["sfirst"]] = first if top_edge else main
        bands[:, g * 7 + BK["slast"]] = last if bot_edge else main
    return bands


def segments():
    return [
        dict(x0=0, nt=32, qlo=0, qhi=32, y0=0, kind="prompt"),
        dict(x0=32, nt=32, qlo=0, qhi=32, y0=32, kind="prompt"),
        dict(x0=64, nt=20, qlo=2, qhi=18, y0=64, kind="sample"),
    ]


def attn_variants(seg, j):
    if seg["kind"] == "prompt":
        last = seg["nt"] - 1
        if j <= 1:
            return [("al", 3, None)]
        if j >= last - 1:
            return [("al", last, None)]
        return [("reg", None)]
    lo, hi = seg["qlo"], seg["qhi"] - 1
    if j <= lo + 1:
        return [("reg", 1), ("al", lo + 3, 0)]
    if j >= hi - 1:
        return [("reg", 3), ("al", hi, 2)]
    return [("reg", None)]


def build_program():
    nc = bass.Bass("TRN2", target_bir_lowering=False)
    segs = segments()
    NXT = sum(s["nt"] for s in segs)
    NYT = sum(s["qhi"] - s["qlo"] for s in segs)

    def din(name, shape, dt):
        return nc.dram_tensor(name, shape, dt, kind="ExternalInput").ap()

    xin = din("xin", [NXT * 128, D], F32)
    w_in_d = din("w_in", [D, 3072], F32)
    w_out_d = din("w_out", [D, D], F32)
    w_pool_d = din("w_pool", [4 * 128, 128], F32)
    ng_d = din("ng", [128, 8], F32)
    psc_d = din("psc", [128, 4], F32)
    gqk_d = din("gqk", [128, 2], F32)
    biasg_d = din("biasg", [8, 128, NTL * 64], F32)
    maskg_d = din("maskg", [128, NTL * 64], BF16)
    bands_d = din("bands", [128, NB * 128], BF16)
    ident_d = din("ident", [128, 128], BF16)
    blk_d = din("blk", [128, 128], BF16)
    ones_d = din("ones64", [128, 64], BF16)
    flags_d = din("flags", [128, 4], F32)
    yout = nc.dram_tensor("yout", [NYT * 128, D], F32, kind="ExternalOutput").ap()

    P = Prog()
    with ExitStack() as ES:
        def SB(name, shape, dt):
            return ES.enter_context(nc.sbuf_tensor(name, shape, dt))

        def SEM(name):
            return ES.enter_context(nc.semaphore(name))

        wib = SB("wib", [128, 8, 3072], BF16)
        wob = SB("wob", [128, 8, 1024], BF16)
        wpb = SB("wpb", [128, 4, 128], BF16)
        etab = SB("etab", [128, 8, NTL, 64], BF16)
        bandb = SB("bandb", [128, NB, 128], BF16)
        ident = SB("ident_s", [128, 128], BF16)
        blk = SB("blk_s", [128, 128], BF16)
        ones64 = SB("ones_s", [128, 64], BF16)
        ng = SB("ng_s", [128, 8], F32)
        psc = SB("psc_s", [128, 4], F32)
        gqk = SB("gqk_s", [128, 2], F32)
        flags = SB("flags_s", [128, 4], F32)
        cst = SB("cst", [128, 4], F32)
        kT = SB("kT", [128, 4, 8, 128], BF16)
        vr = SB("vr", [128, 8, 512], BF16)
        qT = SB("qT", [128, 4, 8, 128], BF16)
        sg = SB("sg", [128, 4, 8, 128], BF16)
        sgp = SB("sgp", [128, 4, 4, 128], BF16)
        mx = SB("mx", [128, 8, 8, 128], BF16)
        utm = SB("utm", [128, 4, 512], BF16)
        hT = SB("hT", [128, 2, 8, 256], BF16)
        xt = SB("xt", [128, 2, 1024], F32)
        xs = SB("xs", [128, 2, 1024], BF16)
        xr = SB("xr", [128, 2, 1024], F32)
        pooled = SB("pooled", [128, 2, 4, 128], BF16)
        sqb = SB("sqb", [128, 4, 256], BF16)
        lnb = SB("lnb", [128, 2, 512], F32)
        kmrg = SB("kmrg", [128, 2, 4, 128], BF16)
        vmrg = SB("vmrg", [128, 2, 512], BF16)
        pet = SB("pet", [128, 4, 512], BF16)
        ptt = SB("ptt", [128, 4, 512], BF16)
        rec = SB("rec", [128, 2, 128], F32)
        tf1 = SB("tf1", [128, 2, 128], F32)
        tf2 = SB("tf2", [128, 128], F32)
        ssq = SB("ssq", [128, 2], F32)
        lsq = SB("lsq", [128, 2], F32)
        rstd = SB("rstd", [128, 2], F32)
        pT = ES.enter_context(nc.psum_tensor("pT", [128, 1024], BF16))
        pm = ES.enter_context(nc.psum_tensor("pm", [128, 7 * 512], F32))

        sems = {e: SEM("s_" + e) for e in COMPUTE}
        dkeys = ["c_ident", "c_blk", "c_ones", "c_ng", "c_psc", "c_gqk", "c_flags", "c_bands", "c_mask",
                 "xt0", "xt1", "xr0", "xr1", "yo0", "yo1"]
        dsem = {k: SEM("d_" + k) for k in dkeys}
        block = ES.enter_context(nc.Block())

        def bank(i, lo=0, hi=512):
            return pm[:, i * 512 + lo:i * 512 + hi]

        def bk(i):
            return [("ps", i, 0), ("ps", i, 1)]

        rot = [0]

        def nb():
            r = rot[0] % 3
            rot[0] += 1
            return r

        def MM(out, lhsT, rhs, start, stop, reads, writes):
            P.add("pe", lambda h: h.matmul(out, lhsT=lhsT, rhs=rhs, start=start, stop=stop), reads=reads, writes=writes)

        def TR(out, in_, reads, writes):
            P.add("pe", lambda h: h.transpose(out=out, in_=in_, identity=ident[:]), reads=reads, writes=writes)

        def ACT(out, in_, func, reads, writes, bias=None, scale=None, accum_out=None):
            kw = {}
            if bias is not None:
                kw["bias"] = bias
            if scale is not None:
                kw["scale"] = scale
            if accum_out is not None:
                kw["accum_out"] = accum_out
            P.add("act", lambda h: h.activation(out=out, in_=in_, func=func, **kw), reads=reads, writes=writes)

        def TT(out, in0, in1, op, reads, writes, eng="dve"):
            P.add(eng, lambda h: h.tensor_tensor(out=out, in0=in0, in1=in1, op=op), reads=reads, writes=writes)

        def TS(out, in0, scalar1, op0, reads, writes, eng="dve"):
            P.add(eng, lambda h: h.tensor_scalar(out=out, in0=in0, scalar1=scalar1, scalar2=None, op0=op0),
                  reads=reads, writes=writes)

        def STT(out, in0, scalar, in1, op0, op1, reads, writes):
            P.add("dve", lambda h: h.scalar_tensor_tensor(out=out, in0=in0, scalar=scalar, in1=in1, op0=op0, op1=op1),
                  reads=reads, writes=writes)

        def CP(out, in_, reads, writes, eng="dve"):
            P.add(eng, lambda h: h.tensor_copy(out=out, in_=in_), reads=reads, writes=writes)

        def RCPF(out, in_, reads, writes):
            P.add("dve", lambda h: h.reciprocal_approx_fast(out=out, in_=in_), reads=reads, writes=writes)

        def DMA(out, in_, reads=(), writes=(), key=None, final=False):
            P.add("sp", lambda h: h.dma_start(out=out, in_=in_), reads=reads, writes=writes, dma_key=key, final=final)

        for key, dst, src in [("c_ident", ident, ident_d), ("c_blk", blk, blk_d), ("c_ones", ones64, ones_d),
                              ("c_ng", ng, ng_d), ("c_psc", psc, psc_d), ("c_gqk", gqk, gqk_d),
                              ("c_flags", flags, flags_d)]:
            DMA(dst[:], src, writes=[key], key=key)
        DMA(bandb[:].rearrange("p a b -> p (a b)"), bands_d, writes=["c_bands"], key="c_bands")
        P.add("dve", lambda h: h.memset(cst[:, 0:1], EPS), writes=["cst"])
        P.add("dve", lambda h: h.memset(cst[:, 1:2], -float(np.log(8.0))), writes=["cst"])
        P.add("dve", lambda h: h.memset(cst[:, 2:3], 0.0), writes=["cst"])
        stgs = [(xr[:, 0, :], "xr0"), (xr[:, 1, :], "xr1"), (xt[:, 0, :], "xt0"), (xt[:, 1, :], "xt1")]
        sctr = [0]

        def next_stg():
            r = stgs[sctr[0] % 4]
            sctr[0] += 1
            return r

        ci = 0
        for c in range(8):
            for t3 in range(3):
                stg, key = next_stg()
                DMA(stg, w_in_d[c * 128:(c + 1) * 128, t3 * 1024:(t3 + 1) * 1024], writes=[key], key=key)
                dst = wib[:, c, t3 * 1024:(t3 + 1) * 1024]
                if ci % 2 == 0:
                    TS(dst, stg, ng[:, c:c + 1], ALU.mult, [key, "c_ng"], [("wib", c, t3)])
                else:
                    TS(dst, stg, ng[:, c:c + 1], ALU.mult, [key, "c_ng"], [("wib", c, t3)])
                ci += 1
        stg, key = next_stg()
        DMA(stg[:, 0:512].rearrange("p (g c) -> p g c", g=4), w_pool_d.rearrange("(g p) c -> p g c", p=128),
            writes=[key], key=key)
        CP(wpb[:].rearrange("p g c -> p (g c)"), stg[:, 0:512], [key], ["wpb"])

        def setup_part2():
            NTH = NTL * 64
            hs = NTH // 2
            petf = pet[:].rearrange("p a b -> p (a b)")
            PETK = [("pet", i) for i in range(4)]
            DMA(petf[:, 0:NTH], maskg_d, writes=PETK, key="c_mask")
            i2 = 0
            for h8 in range(8):
                for hf in range(2):
                    sl = i2 % 2
                    i2 += 1
                    key = "xr%d" % sl
                    sb_ = xr[:, sl, 0:hs]
                    DMA(sb_, biasg_d[h8, :, hf * hs:(hf + 1) * hs], writes=[key], key=key)
                    ACT(sb_, sb_, AF.Exp, [key], [key])
                    TT(etab[:, h8].rearrange("p t c -> p (t c)")[:, hf * hs:(hf + 1) * hs], sb_,
                       petf[:, hf * hs:(hf + 1) * hs], ALU.mult, [key] + PETK, [("etab", h8, hf)])
            for c in range(8):
                sl = i2 % 2
                i2 += 1
                key = "xr%d" % sl
                DMA(xr[:, sl, :], w_out_d[c * 128:(c + 1) * 128, :], writes=[key], key=key)
                CP(wob[:, c, :], xr[:, sl, :], [key], [("wob", c)])

        WIB = [("wib", c, t3) for c in range(8) for t3 in range(3)]
        WOB = [("wob", c) for c in range(8)]
        ETAB = [("etab", h8, hf) for h8 in range(8) for hf in range(2)]

        GB = [(seg, b) for seg in segs for b in range(seg["nt"] // 2)]

        def s8(m):
            return m % 8

        def s4(m):
            return m % 4

        def isq(seg, m):
            return seg["qlo"] <= m < seg["qhi"]

        def stageA0(gi):
            seg, b = GB[gi]
            for ti in range(2):
                m = 2 * b + ti
                sl = m % 2
                row0 = (seg["x0"] + m) * 128
                DMA(xt[:, sl, :], xin[row0:row0 + 128, :], writes=["xt%d" % sl], key="xt%d" % sl)

        def stageD0(gi):
            seg, b = GB[gi]
            for j in (2 * b, 2 * b + 1):
                if not isq(seg, j):
                    continue
                sl = j % 2
                xrow = (seg["x0"] + j) * 128
                DMA(xr[:, sl, :], xin[xrow:xrow + 128, :], writes=["xr%d" % sl], key="xr%d" % sl)

        def stageA(gi):
            seg, b = GB[gi]
            par = gi % 2
            for ti in range(2):
                m = 2 * b + ti
                sl = m % 2
                xk = "xt%d" % sl
                ACT(xs[:, sl, :], xt[:, sl, :], AF.Square, [xk], [("xs", sl), ("ssq", sl)], accum_out=ssq[:, sl:sl + 1])
                ACT(lsq[:, sl:sl + 1], ssq[:, sl:sl + 1], AF.Ln, [("ssq", sl), "cst"], [("lsq", sl)],
                    bias=cst[:, 0:1], scale=1.0 / D)
                ACT(rstd[:, sl:sl + 1], lsq[:, sl:sl + 1], AF.Exp, [("lsq", sl)], [("rstd", sl)], scale=-0.5)
                TS(xs[:, sl, :], xt[:, sl, :], rstd[:, sl:sl + 1], ALU.mult, [xk, ("rstd", sl)], [("xs", sl)])
                yield
                for c in range(8):
                    TR(pT[:, c * 128:(c + 1) * 128], xs[:, sl, c * 128:(c + 1) * 128], [("xs", sl), "c_ident"], ["pT"])
                CP(hT[:, par, :, ti * 128:(ti + 1) * 128], pT[:].rearrange("p (c t) -> p c t", c=8), ["pT"],
                   [("hT", par, ti)])
                yield

        def fm_chunk(gi, fi, kind, col0, i):
            seg, b = GB[gi]
            par = gi % 2
            m0 = 2 * b
            rb_ = nb()
            o = bank(rb_, 0, 256)
            ok = bk(rb_)
            for c in range(8):
                MM(o, wib[:, c, col0:col0 + 128], hT[:, par, c, :], c == 0, c == 7,
                   [("hT", par, 0), ("hT", par, 1)] + WIB, ok)
            if kind == "gp":
                ACT(sgp[:, i, s4(m0):s4(m0) + 2, :].rearrange("p a b -> p (a b)"), o, AF.Silu, ok,
                    [("sgp", i, s4(m0)), ("sgp", i, s4(m0 + 1))])
            elif kind == "ga":
                ACT(sg[:, i, s8(m0):s8(m0) + 2, :].rearrange("p a b -> p (a b)"), o, AF.Silu, ok,
                    [("sg", i, s8(m0)), ("sg", i, s8(m0 + 1))])
            elif kind == "q":
                CP(qT[:, i, s8(m0):s8(m0) + 2, :].rearrange("p a b -> p (a b)"), o, ok,
                   [("qT", i, s8(m0)), ("qT", i, s8(m0 + 1))])
            else:
                ACT(kT[:, i, s8(m0):s8(m0) + 2, :].rearrange("p a b -> p (a b)"), o, AF.Copy, ok,
                    [("kT", i, s8(m0)), ("kT", i, s8(m0 + 1))])

        def stageB_gates(gi):
            seg, b = GB[gi]
            if not (isq(seg, 2 * b) or isq(seg, 2 * b + 1)):
                return
            fm = [("gp", 512 + 128 * i, i) for i in range(4)] + [("ga", 2560 + 128 * i, i) for i in range(4)]
            for fi, (kind, col0, i) in enumerate(fm):
                fm_chunk(gi, fi, kind, col0, i)
                yield

        def qk_keys(gi, which, i):
            seg, b = GB[gi]
            m0 = 2 * b
            if which == 0:
                raw = qT[:, i, s8(m0):s8(m0) + 2, :].rearrange("p a b -> p (a b)")
                keys = [("qT", i, s8(m0)), ("qT", i, s8(m0 + 1))]
            else:
                raw = kT[:, i, s8(m0):s8(m0) + 2, :].rearrange("p a b -> p (a b)")
                keys = [("kT", i, s8(m0)), ("kT", i, s8(m0 + 1))]
            return raw, keys

        def qk_sq(gi, which, i, fi):
            raw, keys = qk_keys(gi, which, i)
            TT(sqb[:, fi % 4, :], raw, raw, ALU.mult, keys, [("sqb", fi % 4)], eng="pool")

        def qk_norm2(gi, which, i0, fi0):
            seg, b = GB[gi]
            m0 = 2 * b
            sl = (fi0 // 2) % 2
            tens = qT if which == 0 else kT
            nm = "qT" if which == 0 else "kT"
            raw = tens[:, i0:i0 + 2, s8(m0):s8(m0) + 2, :].rearrange("p i a b -> p i (a b)")
            keys = [(nm, i, s8(m0 + t)) for i in (i0, i0 + 1) for t in (0, 1)]
            rb_ = nb()
            for t in range(2):
                sq = (fi0 + t) % 4
                MM(bank(rb_, t * 256, t * 256 + 256), blk[:], sqb[:, sq, :], True, True, [("sqb", sq), "c_blk"], bk(rb_))
            ACT(lnb[:, sl, :], bank(rb_), AF.Ln, bk(rb_) + ["cst"], [("lnb", sl)], bias=cst[:, 0:1], scale=1.0 / 64)
            bcol = 1 if which == 0 else 2
            ACT(lnb[:, sl, :], lnb[:, sl, :], AF.Exp, [("lnb", sl), "cst"], [("lnb", sl)], bias=cst[:, bcol:bcol + 1], scale=-0.5)
            STT(raw, raw, gqk[:, which:which + 1], lnb[:, sl, :].rearrange("p (i c) -> p i c", i=2), ALU.mult, ALU.mult,
                keys + [("lnb", sl), "c_gqk"], keys)

        def pool_tile(seg, m):
            nt = seg["nt"]
            if seg["kind"] == "prompt":
                mk_ = "first" if m == 0 else ("last" if m == nt - 1 else "main")
            else:
                mk_ = "sfirst" if m == seg["qlo"] else ("slast" if m == seg["qhi"] - 1 else "main")
            parts = [(m, mk_)]
            if m - 1 >= 0:
                parts.append((m - 1, "prev"))
            if m + 1 < nt:
                parts.append((m + 1, "next"))
            r1 = nb()
            for g in range(4):
                for pi, (mm, kd) in enumerate(parts):
                    c0, c1 = (0, 8) if kd == "prev" else ((120, 128) if kd == "next" else (0, 128))
                    MM(bank(r1, g * 128 + c0, g * 128 + c1), utm[:, s4(mm), g * 128:(g + 1) * 128],
                       bandb[:, g * 7 + BK[kd], c0:c1],
                       pi == 0, pi == len(parts) - 1, [("utm", s4(mm)), "c_bands"], bk(r1))
            ACT(pooled[:, m % 2].rearrange("p g c -> p (g c)"), bank(r1), AF.Copy, bk(r1), [("pooled", m % 2)])

        def pool_tile2(seg, m):
            r2 = nb()
            for g in range(4):
                MM(bank(r2, g * 128, (g + 1) * 128), wpb[:, g, :], pooled[:, m % 2, g, :], True, True,
                   [("pooled", m % 2), "wpb"], bk(r2))
            for g in range(4):
                STT(mx[:, g, s8(m), :], bank(r2, g * 128, (g + 1) * 128), psc[:, g:g + 1], sgp[:, g, s4(m), :],
                    ALU.mult, ALU.mult, bk(r2) + [("sgp", g, s4(m)), "c_psc"], [("mx", g, s8(m))])

        def stageB_rest(gi):
            seg, b = GB[gi]
            par = gi % 2
            m0 = 2 * b
            has_q = isq(seg, m0) or isq(seg, m0 + 1)
            fm = ([("q", 1024 + 128 * i, i) for i in range(4)] if has_q else []) + [("k", 1536 + 128 * i, i) for i in range(4)]

            def need_u(m):
                return any(isq(seg, mm) for mm in (m - 1, m, m + 1))

            def tm(ti, which):
                m = m0 + ti
                col0 = 0 if which == 0 else 2048
                rb_ = nb()
                for c in range(8):
                    MM(bank(rb_), hT[:, par, c, ti * 128:(ti + 1) * 128], wib[:, c, col0:col0 + 512], c == 0, c == 7,
                       [("hT", par, ti)] + WIB, bk(rb_))
                if which == 0:
                    CP(utm[:, s4(m), :], bank(rb_), bk(rb_), [("utm", s4(m))])
                else:
                    ACT(vr[:, s8(m), :], bank(rb_), AF.Copy, bk(rb_), [("vr", s8(m))])

            if need_u(m0):
                tm(0, 0)
                yield
            if need_u(m0 + 1):
                tm(1, 0)
                yield
            p2q = []
            ptiles = [m for m in (m0 - 1, m0) if m >= 0 and isq(seg, m)]
            if m0 + 2 == seg["nt"] and isq(seg, m0 + 1):
                ptiles.append(m0 + 1)
            for fi, (kind, col0, i) in enumerate(fm):
                fm_chunk(gi, fi, kind, col0, i)
                yield
                if fi == 3:
                    tm(0, 1)
                    yield
                if p2q and fi in (2, 5):
                    pool_tile2(seg, p2q.pop(0))
                    yield
                if fi in (1, 4) and ptiles:
                    mp = ptiles.pop(0)
                    pool_tile(seg, mp)
                    p2q.append(mp)
                    yield
                qk_sq(gi, 0 if kind == "q" else 1, i, fi)
                if fi >= 3 and fi % 2 == 1:
                    kind2, _, i2 = fm[fi - 3]
                    qk_norm2(gi, 0 if kind2 == "q" else 1, i2, fi - 3)
            nf = len(fm)
            yield
            tm(1, 1)
            kind2, _, i2 = fm[nf - 2]
            qk_norm2(gi, 0 if kind2 == "q" else 1, i2, nf - 2)
            yield
            while ptiles or p2q:
                if p2q:
                    pool_tile2(seg, p2q.pop(0))
                    yield
                    continue
                mp = ptiles.pop(0)
                pool_tile(seg, mp)
                p2q.append(mp)
                yield
                continue
                yield

        it_ctr = [0]

        def stageC(gi):
            seg, b = GB[gi]
            pend = None
            for j in (2 * b, 2 * b + 1):
                if not isq(seg, j):
                    continue
                variants = attn_variants(seg, j)
                jp = j % 2
                if any(v[0] == "reg" for v in variants):
                    ACT(kmrg[:, jp, :, 0:64], kT[:, :, s8(j + 2), 0:64], AF.Copy, [("kT", hp, s8(j + 2)) for hp in range(4)],
                        [("kmrg", jp)])
                    ACT(kmrg[:, jp, :, 64:128], kT[:, :, s8(j - 2), 64:128], AF.Copy, [("kT", hp, s8(j - 2)) for hp in range(4)],
                        [("kmrg", jp)])
                    ACT(vmrg[0:64, jp, :], vr[0:64, s8(j + 2), :], AF.Copy, [("vr", s8(j + 2))], [("vmrg", jp)])
                    ACT(vmrg[64:128, jp, :], vr[64:128, s8(j - 2), :], AF.Copy, [("vr", s8(j - 2))], [("vmrg", jp)])
                for hp in range(4):
                    for vi, var in enumerate(variants):
                        st = it_ctr[0] % 2
                        it_ctr[0] += 1
                        fl = var[-1]
                        hA, hB = 2 * hp, 2 * hp + 1
                        qk = ("qT", hp, s8(j))
                        if var[0] == "reg":
                            e0 = 0
                            blocks = [(("kT", j + 1), 0, 128, 0), (("kT", j), 0, 128, 128), (("kT", j - 1), 0, 128, 256),
                                      (("kT", j - 2), 0, 64, 384), (("mrg", jp), 64, 128, 448)]
                        else:
                            kb_hi = var[1]
                            e0 = NT_REG + 6 - 2 * (kb_hi - j)
                            blocks = [(("kT", kb_hi - i), 0, 128, 128 * i) for i in range(4)]
                        for (src, ql, qh, sc) in blocks:
                            n = qh - ql
                            for hh in range(2):
                                r0 = 64 * hh
                                if src[0] == "kT":
                                    lhs = kT[r0:r0 + 64, hp, s8(src[1]), :]
                                    rk = ("kT", hp, s8(src[1]))
                                else:
                                    lhs = kmrg[r0:r0 + 64, src[1], hp, :]
                                    rk = ("kmrg", src[1])
                                MM(bank(3 + hh, sc, sc + n), lhs, qT[r0:r0 + 64, hp, s8(j), ql:qh], True, True,
                                   [rk, qk], bk(3 + hh))
                        slots = [2 * st, 2 * st + 1]
                        pk = [("pet", slots[0]), ("pet", slots[1])]
                        tk = [("ptt", slots[0]), ("ptt", slots[1])]
                        ACT(pet[:, 2 * st:2 * st + 2, :].rearrange("p a b -> p (a b)"), pm[:, 3 * 512:5 * 512], AF.Exp,
                            bk(3) + bk(4), pk)
                        TT(ptt[:, 2 * st:2 * st + 2, :], pet[:, 2 * st:2 * st + 2, :],
                           etab[:, hA:hA + 2, e0:e0 + 8, :].rearrange("p h t c -> p h (t c)"), ALU.mult,
                           pk + ETAB, tk)
                        yield
                        if pend is not None:
                            pend()
                            yield

                        def pv(st=st, slots=slots, blocks=blocks, hA=hA, hB=hB, fl=fl, vi=vi, hp=hp, j=j):
                            ndb = 5 + st
                            for grp in range(2):
                                ob = ndb * 512 + grp * 128
                                for ii, (src, ql, qh, sc) in enumerate(blocks):
                                    n = qh - ql
                                    for hh, hd in enumerate([hA, hB]):
                                        po = 64 * hh
                                        if grp == 1:
                                            lhs, rds = ones64[:], ["c_ones"]
                                        elif src[0] == "kT":
                                            lhs, rds = vr[:, s8(src[1]), hd * 64:(hd + 1) * 64], [("vr", s8(src[1]))]
                                        else:
                                            lhs, rds = vmrg[:, src[1], hd * 64:(hd + 1) * 64], [("vmrg", src[1])]
                                        MM(pm[po:po + 64, ob + ql:ob + qh], lhs, ptt[:, slots[hh], sc:sc + n],
                                           ii == 0, ii == len(blocks) - 1, rds + [("ptt", slots[hh])], bk(ndb))
                            num = bank(ndb, 0, 128)
                            den = bank(ndb, 128, 256)
                            ACT(rec[:, st, :], den, AF.Ln, bk(ndb), [("rec", st)])
                            ACT(rec[:, st, :], rec[:, st, :], AF.Exp, [("rec", st)], [("rec", st)], scale=-1.0)
                            if fl is None:
                                TT(tf1[:, st, :], num, rec[:, st, :], ALU.mult, bk(ndb) + [("rec", st)], [("tf1", st)])
                            else:
                                STT(tf1[:, st, :], num, flags[:, fl:fl + 1], rec[:, st, :], ALU.mult, ALU.mult,
                                    bk(ndb) + [("rec", st), "c_flags"], [("tf1", st)])
                            mk_ = ("mx", 4 + hp, s8(j))
                            if vi == 0:
                                TT(mx[:, 4 + hp, s8(j), :], tf1[:, st, :], sg[:, hp, s8(j), :], ALU.mult,
                                   [("tf1", st), ("sg", hp, s8(j))], [mk_])
                            else:
                                TT(tf2[:], tf1[:, st, :], sg[:, hp, s8(j), :], ALU.mult,
                                   [("tf1", st), ("sg", hp, s8(j))], ["tf2"])
                                TT(mx[:, 4 + hp, s8(j), :], mx[:, 4 + hp, s8(j), :], tf2[:], ALU.add, ["tf2", mk_], [mk_])
                        pend = pv
            if pend is not None:
                pend()
                yield

        def stageD(gi):
            seg, b = GB[gi]
            for j in (2 * b, 2 * b + 1):
                if not isq(seg, j):
                    continue
                sl = j % 2
                xrow = (seg["x0"] + j) * 128
                yrow = (seg["y0"] + j - seg["qlo"]) * 128
                xk = "xr%d" % sl
                for half in range(2):
                    rb_ = nb()
                    for c in range(8):
                        MM(bank(rb_), mx[:, c, s8(j), :], wob[:, c, half * 512:(half + 1) * 512], c == 0, c == 7,
                           [("mx", c, s8(j))] + WOB, bk(rb_))
                    TT(xr[:, sl, half * 512:(half + 1) * 512], bank(rb_), xr[:, sl, half * 512:(half + 1) * 512], ALU.add,
                       bk(rb_) + [xk], [xk])
                    yield
                DMA(yout[yrow:yrow + 128, :], xr[:, sl, :], reads=[xk], key="yo%d" % sl, final=True)

        def drain(g):
            for _ in g:
                pass

        def merge(gens):
            gens = [[g, 0, max(u, 1)] for g, u in gens]
            while gens:
                gens.sort(key=lambda t: (t[1] + 1) / t[2])
                t = gens[0]
                try:
                    next(t[0])
                    t[1] += 1
                except StopIteration:
                    gens.remove(t)

        NG = len(GB)
        import os
        NIT = int(os.environ.get("DBG_NIT", NG + 3))
        STG = os.environ.get("DBG_STAGES", "AGBCD")
        stageA0(0)
        drain(stageA(0))
        stageA0(1)
        for k in range(0, min(NG + 3, NIT)):
            if 0 <= k - 3 < NG:
                stageD0(k - 3)
            if k < NG and "G" in STG:
                drain(stageB_gates(k))
            gl = []
            if k < NG and "B" in STG:
                gl.append((stageB_rest(k), 16))
            if 0 <= k - 2 < NG and "C" in STG:
                gl.append((stageC(k - 2), int(os.environ.get('K_CW', 14))))
            if 0 <= k - 3 < NG and "D" in STG:
                gl.append((stageD(k - 3), 4))
            if k + 1 < NG and "A" in STG:
                gl.append((stageA(k + 1), 4))
            merge(gl)
            if k + 2 < NG:
                stageA0(k + 2)
            if k == 0:
                setup_part2()

        P.emit(block, sems, dsem)
    return nc


_NC_CACHE = {}


def kernel(x_prompt, x_sample, norm_g, w_in, w_pool, pool_scale, q_norm_g, k_norm_g, rpb, w_out):
    x_prompt = np.asarray(x_prompt, np.float32)
    x_sample = np.asarray(x_sample, np.float32)
    if "nc" not in _NC_CACHE:
        _NC_CACHE["nc"] = build_program()
    nc = _NC_CACHE["nc"]
    biasg, maskg = build_bias_mask(np.asarray(rpb, np.float32)[0])
    ident = np.eye(128, dtype=np.float32).astype(bf)
    blk = np.kron(np.eye(2, dtype=np.float32), np.ones((64, 64), np.float32)).astype(bf)
    ones64 = np.ones((128, 64), np.float32).astype(bf)
    ng = np.ascontiguousarray(np.asarray(norm_g, np.float32)[0].reshape(8, 128).T)
    psc = np.ascontiguousarray(np.asarray(pool_scale, np.float32)[0].reshape(4, 128).T)
    gq = np.asarray(q_norm_g, np.float32)[0]
    gk = np.asarray(k_norm_g, np.float32)[0]
    gqk = np.ascontiguousarray(np.stack([np.concatenate([gq, gq]), np.concatenate([gk, gk])], axis=1))
    w_in0 = np.ascontiguousarray(np.asarray(w_in, np.float32)[0])
    w_out0 = np.ascontiguousarray(np.asarray(w_out, np.float32)[0])
    w_pool0 = np.ascontiguousarray(np.asarray(w_pool, np.float32)[0].reshape(512, 128))
    in_maps = []
    for c in range(NCORES):
        sb, sq = c // 4, c % 4
        r0, r1 = sq * 2048 - 256, sq * 2048 + 2048 + 256
        piece = np.zeros((2560, D), np.float32)
        a0, a1 = max(r0, 0), min(r1, 8192)
        piece[a0 - r0:a1 - r0] = x_sample[sb, a0:a1]
        xin = np.concatenate([x_prompt[2 * c], x_prompt[2 * c + 1], piece], axis=0)
        top, bot = (sq == 0), (sq == 3)
        fl = np.tile(np.array([[float(top), 1.0 - float(top), float(bot), 1.0 - float(bot)]], np.float32), (128, 1))
        bands = build_bands(top, bot).reshape(128, NB * 128).astype(bf)
        in_maps.append(dict(xin=np.ascontiguousarray(xin), w_in=w_in0, w_out=w_out0, w_pool=w_pool0, ng=ng, psc=psc,
                            gqk=gqk, biasg=biasg, maskg=maskg.astype(bf), bands=bands, ident=ident, blk=blk, ones64=ones64,
                            flags=fl))
    res = run_bass_kernel_spmd(nc, in_maps, core_ids=list(range(NCORES)))
    y_prompt = np.empty((16, 4096, D), np.float32)
    y_sample = np.empty((2, 8192, D), np.float32)
    for c in range(NCORES):
        y = np.asarray(res.results[c]["yout"])
        y_prompt[2 * c] = y[0:4096]
        y_prompt[2 * c + 1] = y[4096:8192]
        y_sample[c // 4, (c % 4) * 2048:(c % 4 + 1) * 2048] = y[8192:10240]
    return (y_prompt, y_sample)
```

```python
import os
import numpy as np
import ml_dtypes
from contextlib import ExitStack
import concourse.bass as bass
import concourse.mybir as mybir
from concourse.bass_utils import run_bass_kernel_spmd

F32 = mybir.dt.float32
BF16 = mybir.dt.bfloat16
AF = mybir.ActivationFunctionType
ALU = mybir.AluOpType
bf = ml_dtypes.bfloat16

NCORES = 8
D = 1024
EPS = 1e-6
NT_REG, NT_ALL = 8, 14
NTL = NT_REG + NT_ALL
POOL_W = (2, 4, 8, 16)
BK = {"main": 0, "prev": 1, "next": 2, "first": 3, "last": 4, "sfirst": 5, "slast": 6}
NB = 28

COMPUTE = ("pe", "act", "dve", "pool")


class Op:
    __slots__ = ("eng", "fn", "deps", "idx", "need_inc", "inc_no", "dma_key", "dma_cnt", "is_dma")

    def __init__(self, eng, fn):
        self.eng = eng
        self.fn = fn
        self.deps = []
        self.need_inc = False
        self.inc_no = 0
        self.is_dma = False
        self.dma_key = None
        self.dma_cnt = 0


class Prog:
    def __init__(self, same_engine_sync=True):
        self.ops = {e: [] for e in COMPUTE + ("sp", "gq")}
        self.last_w = {}
        self.readers = {}
        self.dma_cnt = {}
        self.same_engine_sync = same_engine_sync
        self.final_dma = []

    def _dep(self, op, other):
        if other is None or other is op:
            return
        if other.eng == op.eng and not other.is_dma:
            if op.eng == "pe" or not self.same_engine_sync:
                return
        if other not in op.deps:
            op.deps.append(other)
            if not other.is_dma:
                other.need_inc = True

    def add(self, eng, fn, reads=(), writes=(), dma_key=None, final=False):
        op = Op(eng, fn)
        if dma_key is not None:
            op.is_dma = True
            op.dma_key = dma_key
            self.dma_cnt[dma_key] = self.dma_cnt.get(dma_key, 0) + 16
            op.dma_cnt = self.dma_cnt[dma_key]
            if final:
                self.final_dma.append(op)
        for r in reads:
            self._dep(op, self.last_w.get(r))
        for w in writes:
            self._dep(op, self.last_w.get(w))
            for rd in self.readers.get(w, ()):
                self._dep(op, rd)
        for r in reads:
            self.readers.setdefault(r, []).append(op)
        for w in writes:
            self.last_w[w] = op
            self.readers[w] = []
        self.ops[eng].append(op)
        return op

    def emit(self, block, sems, dma_sems):
        for e in COMPUTE:
            n = 0
            for op in self.ops[e]:
                if op.need_inc:
                    n += 1
                    op.inc_no = n

        def run(eng_name, h):
            waited = {}
            for op in self.ops[eng_name]:
                for d in op.deps:
                    if d.is_dma:
                        s, v = dma_sems[d.dma_key], d.dma_cnt
                    else:
                        s, v = sems[d.eng], d.inc_no
                    k = id(s)
                    if waited.get(k, 0) < v:
                        h.wait_ge(s, v)
                        waited[k] = v
                ins = op.fn(h)
                if op.is_dma:
                    ins.then_inc(dma_sems[op.dma_key], 16)
                elif op.need_inc:
                    ins.then_inc(sems[op.eng], 1)
            if eng_name == "sp":
                for op in self.final_dma:
                    s, v = dma_sems[op.dma_key], op.dma_cnt
                    if waited.get(id(s), 0) < v:
                        h.wait_ge(s, v)
                        waited[id(s)] = v

        @block.tensor
        def _(h):
            run("pe", h)

        @block.scalar
        def _(h):
            run("act", h)

        @block.vector
        def _(h):
            run("dve", h)

        @block.gpsimd
        def _(h):
            run("pool", h)

        @block.sync
        def _(h):
            run("sp", h)


def tile_specs():
    reg = [(d, d + 1) for d in (2, 1, 0, -1, -2, -3, -4)] + [(3, -4)]
    al = [(d if abs(d) <= 7 else None, d + 1 if abs(d + 1) <= 7 else None) for d in range(6, -8, -1)]
    return reg + al


def build_bias_mask(rpb):
    kc = np.arange(64)[:, None]
    qc = np.arange(64)[None, :]
    cs = np.clip(qc - 8, 0, 48)
    colmask = ((kc >= cs) & (kc < cs + 16)).astype(np.float32)
    dc = np.clip(kc - qc + 15, 0, 30)
    biasg = np.zeros((8, 128, NTL * 64), np.float32)
    maskg = np.zeros((128, NTL * 64), np.float32)
    for ti, drs in enumerate(tile_specs()):
        for kr in range(2):
            dr = drs[kr]
            if dr is None:
                continue
            biasg[:, kr * 64:(kr + 1) * 64, ti * 64:(ti + 1) * 64] = rpb[:, dr + 7][:, dc]
            maskg[kr * 64:(kr + 1) * 64, ti * 64:(ti + 1) * 64] = colmask
    return biasg, maskg


def build_bands(top_edge, bot_edge):
    bands = np.zeros((128, NB, 128), np.float32)
    t = np.arange(128)[:, None]
    tp = np.arange(128)[None, :]
    for g, w in enumerate(POOL_W):
        a = w // 2
        b = w - a
        lo, hi = tp - a, tp + b
        eye = (t == tp).astype(np.float32)
        main = ((t >= lo) & (t < hi)).astype(np.float32) / w - eye
        prev = ((t - 128 >= lo) & (t - 128 < hi)).astype(np.float32) / w
        nxt = ((t + 128 >= lo) & (t + 128 < hi)).astype(np.float32) / w
        cnt_f = (hi - np.maximum(lo, 0)).astype(np.float32)
        first = ((t >= lo) & (t < hi)).astype(np.float32) / cnt_f - eye
        cnt_l = (np.minimum(hi, 128) - lo).astype(np.float32)
        last = ((t >= lo) & (t < hi)).astype(np.float32) / cnt_l - eye
        bands[:, g * 7 + BK["main"]] = main
        bands[:, g * 7 + BK["prev"]] = prev
        bands[:, g * 7 + BK["next"]] = nxt
        bands[:, g * 7 + BK["first"]] = first
        bands[:, g * 7 + BK["last"]] = last
        bands[:, g * 7 + BK["sfirst"]] = first if top_edge else main
        bands[:, g * 7 + BK["slast"]] = last if bot_edge else main
    return bands


def segments():
    return [
        dict(x0=0, nt=32, qlo=0, qhi=32, y0=0, kind="prompt"),
        dict(x0=32, nt=32, qlo=0, qhi=32, y0=32, kind="prompt"),
        dict(x0=64, nt=20, qlo=2, qhi=18, y0=64, kind="sample"),
    ]


def attn_variants(seg, j):
    if seg["kind"] == "prompt":
        last = seg["nt"] - 1
        if j <= 1:
            return [("al", 3, None)]
        if j >= last - 1:
            return [("al", last, None)]
        return [("reg", None)]
    lo, hi = seg["qlo"], seg["qhi"] - 1
    if j <= lo + 1:
        return [("reg", 1), ("al", lo + 3, 0)]
    if j >= hi - 1:
        return [("reg", 3), ("al", hi, 2)]
    return [("reg", None)]


def build_program():
    nc = bass.Bass("TRN2", target_bir_lowering=False)
    segs = segments()
    NXT = sum(s["nt"] for s in segs)
    NYT = sum(s["qhi"] - s["qlo"] for s in segs)

    def din(name, shape, dt):
        return nc.dram_tensor(name, shape, dt, kind="ExternalInput").ap()

    xin = din("xin", [NXT * 128, D], F32)
    w_in_d = din("w_in", [D, 3072], F32)
    w_out_d = din("w_out", [D, D], F32)
    w_pool_d = din("w_pool", [4 * 128, 128], F32)
    ng_d = din("ng", [128, 8], F32)
    psc_d = din("psc", [128, 4], F32)
    gqk_d = din("gqk", [128, 2], F32)
    biasg_d = din("biasg", [8, 128, NTL * 64], F32)
    maskg_d = din("maskg", [128, NTL * 64], BF16)
    bands_d = din("bands", [128, NB * 128], BF16)
    ident_d = din("ident", [128, 128], BF16)
    blk_d = din("blk", [128, 128], BF16)
    ones_d = din("ones64", [128, 64], BF16)
    flags_d = din("flags", [128, 4], F32)
    yout = nc.dram_tensor("yout", [NYT * 128, D], F32, kind="ExternalOutput").ap()

    P = Prog()
    with ExitStack() as ES:
        def SB(name, shape, dt):
            return ES.enter_context(nc.sbuf_tensor(name, shape, dt))

        def SEM(name):
            return ES.enter_context(nc.semaphore(name))

        wib = SB("wib", [128, 8, 3072], BF16)
        wob = SB("wob", [128, 8, 1024], BF16)
        wpb = SB("wpb", [128, 4, 128], BF16)
        etab = SB("etab", [128, 8, NTL, 64], BF16)
        bandb = SB("bandb", [128, NB, 128], BF16)
        ident = SB("ident_s", [128, 128], BF16)
        blk = SB("blk_s", [128, 128], BF16)
        ones64 = SB("ones_s", [128, 64], BF16)
        ng = SB("ng_s", [128, 8], F32)
        psc = SB("psc_s", [128, 4], F32)
        gqk = SB("gqk_s", [128, 2], F32)
        flags = SB("flags_s", [128, 4], F32)
        cst = SB("cst", [128, 4], F32)
        kT = SB("kT", [128, 4, 8, 128], BF16)
        vr = SB("vr", [128, 8, 512], BF16)
        qT = SB("qT", [128, 4, 8, 128], BF16)
        sg = SB("sg", [128, 4, 8, 128], BF16)
        sgp = SB("sgp", [128, 4, 4, 128], BF16)
        mx = SB("mx", [128, 8, 8, 128], BF16)
        utm = SB("utm", [128, 4, 512], BF16)
        hT = SB("hT", [128, 2, 8, 256], BF16)
        xt = SB("xt", [128, 2, 1024], F32)
        xs = SB("xs", [128, 2, 1024], BF16)
        xr = SB("xr", [128, 2, 1024], F32)
        pooled = SB("pooled", [128, 2, 4, 128], BF16)
        sqb = SB("sqb", [128, 4, 256], BF16)
        lnb = SB("lnb", [128, 2, 512], F32)
        kmrg = SB("kmrg", [128, 2, 4, 128], BF16)
        vmrg = SB("vmrg", [128, 2, 512], BF16)
        pet = SB("pet", [128, 4, 512], BF16)
        ptt = SB("ptt", [128, 4, 512], BF16)
        rec = SB("rec", [128, 2, 128], F32)
        tf1 = SB("tf1", [128, 2, 128], F32)
        tf2 = SB("tf2", [128, 128], F32)
        ssq = SB("ssq", [128, 2], F32)
        lsq = SB("lsq", [128, 2], F32)
        rstd = SB("rstd", [128, 2], F32)
        pT = ES.enter_context(nc.psum_tensor("pT", [128, 1024], BF16))
        pm = ES.enter_context(nc.psum_tensor("pm", [128, 7 * 512], F32))

        sems = {e: SEM("s_" + e) for e in COMPUTE}
        dkeys = ["c_ident", "c_blk", "c_ones", "c_ng", "c_psc", "c_gqk", "c_flags", "c_bands", "c_mask",
                 "xt0", "xt1", "xr0", "xr1", "yo0", "yo1"]
        dsem = {k: SEM("d_" + k) for k in dkeys}
        block = ES.enter_context(nc.Block())

        def bank(i, lo=0, hi=512):
            return pm[:, i * 512 + lo:i * 512 + hi]

        def bk(i):
            return [("ps", i, 0), ("ps", i, 1)]

        rot = [0]

        def nb():
            r = rot[0] % 3
            rot[0] += 1
            return r

        def MM(out, lhsT, rhs, start, stop, reads, writes):
            P.add("pe", lambda h: h.matmul(out, lhsT=lhsT, rhs=rhs, start=start, stop=stop), reads=reads, writes=writes)

        def TR(out, in_, reads, writes):
            P.add("pe", lambda h: h.transpose(out=out, in_=in_, identity=ident[:]), reads=reads, writes=writes)

        def ACT(out, in_, func, reads, writes, bias=None, scale=None, accum_out=None):
            kw = {}
            if bias is not None:
                kw["bias"] = bias
            if scale is not None:
                kw["scale"] = scale
            if accum_out is not None:
                kw["accum_out"] = accum_out
            P.add("act", lambda h: h.activation(out=out, in_=in_, func=func, **kw), reads=reads, writes=writes)

        def TT(out, in0, in1, op, reads, writes, eng="dve"):
            P.add(eng, lambda h: h.tensor_tensor(out=out, in0=in0, in1=in1, op=op), reads=reads, writes=writes)

        def TS(out, in0, scalar1, op0, reads, writes, eng="dve"):
            P.add(eng, lambda h: h.tensor_scalar(out=out, in0=in0, scalar1=scalar1, scalar2=None, op0=op0),
                  reads=reads, writes=writes)

        def STT(out, in0, scalar, in1, op0, op1, reads, writes):
            P.add("dve", lambda h: h.scalar_tensor_tensor(out=out, in0=in0, scalar=scalar, in1=in1, op0=op0, op1=op1),
                  reads=reads, writes=writes)

        def CP(out, in_, reads, writes, eng="dve"):
            P.add(eng, lambda h: h.tensor_copy(out=out, in_=in_), reads=reads, writes=writes)

        def RCPF(out, in_, reads, writes):
            P.add("dve", lambda h: h.reciprocal_approx_fast(out=out, in_=in_), reads=reads, writes=writes)

        def DMA(out, in_, reads=(), writes=(), key=None, final=False):
            P.add("sp", lambda h: h.dma_start(out=out, in_=in_), reads=reads, writes=writes, dma_key=key, final=final)

        for key, dst, src in [("c_ident", ident, ident_d), ("c_blk", blk, blk_d), ("c_ones", ones64, ones_d),
                              ("c_ng", ng, ng_d), ("c_psc", psc, psc_d), ("c_gqk", gqk, gqk_d),
                              ("c_flags", flags, flags_d)]:
            DMA(dst[:], src, writes=[key], key=key)
        DMA(bandb[:].rearrange("p a b -> p (a b)"), bands_d, writes=["c_bands"], key="c_bands")
        P.add("dve", lambda h: h.memset(cst[:, 0:1], EPS), writes=["cst"])
        P.add("dve", lambda h: h.memset(cst[:, 1:2], -float(np.log(8.0))), writes=["cst"])
        P.add("dve", lambda h: h.memset(cst[:, 2:3], 0.0), writes=["cst"])
        stgs = [(xr[:, 0, :], "xr0"), (xr[:, 1, :], "xr1"), (xt[:, 0, :], "xt0"), (xt[:, 1, :], "xt1")]
        sctr = [0]

        def next_stg():
            r = stgs[sctr[0] % 4]
            sctr[0] += 1
            return r

        ci = 0
        for c in range(8):
            for t3 in range(3):
                stg, key = next_stg()
                DMA(stg, w_in_d[c * 128:(c + 1) * 128, t3 * 1024:(t3 + 1) * 1024], writes=[key], key=key)
                dst = wib[:, c, t3 * 1024:(t3 + 1) * 1024]
                if ci % 2 == 0:
                    TS(dst, stg, ng[:, c:c + 1], ALU.mult, [key, "c_ng"], [("wib", c, t3)])
                else:
                    TS(dst, stg, ng[:, c:c + 1], ALU.mult, [key, "c_ng"], [("wib", c, t3)])
                ci += 1
        stg, key = next_stg()
        DMA(stg[:, 0:512].rearrange("p (g c) -> p g c", g=4), w_pool_d.rearrange("(g p) c -> p g c", p=128),
            writes=[key], key=key)
        CP(wpb[:].rearrange("p g c -> p (g c)"), stg[:, 0:512], [key], ["wpb"])

        def setup_part2():
            NTH = NTL * 64
            hs = NTH // 2
            petf = pet[:].rearrange("p a b -> p (a b)")
            PETK = [("pet", i) for i in range(4)]
            DMA(petf[:, 0:NTH], maskg_d, writes=PETK, key="c_mask")
            i2 = 0
            for h8 in range(8):
                for hf in range(2):
                    sl = i2 % 2
                    i2 += 1
                    key = "xr%d" % sl
                    sb_ = xr[:, sl, 0:hs]
                    DMA(sb_, biasg_d[h8, :, hf * hs:(hf + 1) * hs], writes=[key], key=key)
                    ACT(sb_, sb_, AF.Exp, [key], [key])
                    TT(etab[:, h8].rearrange("p t c -> p (t c)")[:, hf * hs:(hf + 1) * hs], sb_,
                       petf[:, hf * hs:(hf + 1) * hs], ALU.mult, [key] + PETK, [("etab", h8, hf)])
            for c in range(8):
                sl = i2 % 2
                i2 += 1
                key = "xr%d" % sl
                DMA(xr[:, sl, :], w_out_d[c * 128:(c + 1) * 128, :], writes=[key], key=key)
                CP(wob[:, c, :], xr[:, sl, :], [key], [("wob", c)])

        WIB = [("wib", c, t3) for c in range(8) for t3 in range(3)]
        WOB = [("wob", c) for c in range(8)]
        ETAB = [("etab", h8, hf) for h8 in range(8) for hf in range(2)]

        GB = [(seg, b) for seg in segs for b in range(seg["nt"] // 2)]

        def s8(m):
            return m % 8

        def s4(m):
            return m % 4

        def isq(seg, m):
            return seg["qlo"] <= m < seg["qhi"]

        def stageA0(gi):
            seg, b = GB[gi]
            for ti in range(2):
                m = 2 * b + ti
                sl = m % 2
                row0 = (seg["x0"] + m) * 128
                DMA(xt[:, sl, :], xin[row0:row0 + 128, :], writes=["xt%d" % sl], key="xt%d" % sl)

        def stageD0(gi):
            seg, b = GB[gi]
            for j in (2 * b, 2 * b + 1):
                if not isq(seg, j):
                    continue
                sl = j % 2
                xrow = (seg["x0"] + j) * 128
                DMA(xr[:, sl, :], xin[xrow:xrow + 128, :], writes=["xr%d" % sl], key="xr%d" % sl)

        def stageA(gi):
            seg, b = GB[gi]
            par = gi % 2
            for ti in range(2):
                m = 2 * b + ti
                sl = m % 2
                xk = "xt%d" % sl
                ACT(xs[:, sl, :], xt[:, sl, :], AF.Square, [xk], [("xs", sl), ("ssq", sl)], accum_out=ssq[:, sl:sl + 1])
                ACT(lsq[:, sl:sl + 1], ssq[:, sl:sl + 1], AF.Ln, [("ssq", sl), "cst"], [("lsq", sl)],
                    bias=cst[:, 0:1], scale=1.0 / D)
                ACT(rstd[:, sl:sl + 1], lsq[:, sl:sl + 1], AF.Exp, [("lsq", sl)], [("rstd", sl)], scale=-0.5)
                TS(xs[:, sl, :], xt[:, sl, :], rstd[:, sl:sl + 1], ALU.mult, [xk, ("rstd", sl)], [("xs", sl)])
                yield
                for c in range(8):
                    TR(pT[:, c * 128:(c + 1) * 128], xs[:, sl, c * 128:(c + 1) * 128], [("xs", sl), "c_ident"], ["pT"])
                CP(hT[:, par, :, ti * 128:(ti + 1) * 128], pT[:].rearrange("p (c t) -> p c t", c=8), ["pT"],
                   [("hT", par, ti)])
                yield

        def fm_chunk(gi, fi, kind, col0, i):
            seg, b = GB[gi]
            par = gi % 2
            m0 = 2 * b
            rb_ = nb()
            o = bank(rb_, 0, 256)
            ok = bk(rb_)
            for c in range(8):
                MM(o, wib[:, c, col0:col0 + 128], hT[:, par, c, :], c == 0, c == 7,
                   [("hT", par, 0), ("hT", par, 1)] + WIB, ok)
            if kind == "gp":
                ACT(sgp[:, i, s4(m0):s4(m0) + 2, :].rearrange("p a b -> p (a b)"), o, AF.Silu, ok,
                    [("sgp", i, s4(m0)), ("sgp", i, s4(m0 + 1))])
            elif kind == "ga":
                ACT(sg[:, i, s8(m0):s8(m0) + 2, :].rearrange("p a b -> p (a b)"), o, AF.Silu, ok,
                    [("sg", i, s8(m0)), ("sg", i, s8(m0 + 1))])
            elif kind == "q":
                CP(qT[:, i, s8(m0):s8(m0) + 2, :].rearrange("p a b -> p (a b)"), o, ok,
                   [("qT", i, s8(m0)), ("qT", i, s8(m0 + 1))])
            else:
                ACT(kT[:, i, s8(m0):s8(m0) + 2, :].rearrange("p a b -> p (a b)"), o, AF.Copy, ok,
                    [("kT", i, s8(m0)), ("kT", i, s8(m0 + 1))])

        def stageB_gates(gi):
            seg, b = GB[gi]
            if not (isq(seg, 2 * b) or isq(seg, 2 * b + 1)):
                return
            par = gi % 2
            m0 = 2 * b
            for kind, cbase in (("gp", 512), ("ga", 2560)):
                for i0 in (0, 2):
                    rb_ = nb()
                    for t in range(2):
                        col0 = cbase + 128 * (i0 + t)
                        for c in range(8):
                            MM(bank(rb_, t * 256, t * 256 + 256), wib[:, c, col0:col0 + 128], hT[:, par, c, :],
                               c == 0, c == 7, [("hT", par, 0), ("hT", par, 1)] + WIB, bk(rb_))
                        yield
                    src = bank(rb_).rearrange("p (i c) -> p i c", i=2)
                    if kind == "gp":
                        dst = sgp[:, i0:i0 + 2, s4(m0):s4(m0) + 2, :].rearrange("p i a b -> p i (a b)")
                        keys = [("sgp", i, s4(m0 + t)) for i in (i0, i0 + 1) for t in (0, 1)]
                    else:
                        dst = sg[:, i0:i0 + 2, s8(m0):s8(m0) + 2, :].rearrange("p i a b -> p i (a b)")
                        keys = [("sg", i, s8(m0 + t)) for i in (i0, i0 + 1) for t in (0, 1)]
                    ACT(dst, src, AF.Silu, bk(rb_), keys)

        def qk_keys(gi, which, i):
            seg, b = GB[gi]
            m0 = 2 * b
            if which == 0:
                raw = qT[:, i, s8(m0):s8(m0) + 2, :].rearrange("p a b -> p (a b)")
                keys = [("qT", i, s8(m0)), ("qT", i, s8(m0 + 1))]
            else:
                raw = kT[:, i, s8(m0):s8(m0) + 2, :].rearrange("p a b -> p (a b)")
                keys = [("kT", i, s8(m0)), ("kT", i, s8(m0 + 1))]
            return raw, keys

        def qk_sq(gi, which, i, fi):
            raw, keys = qk_keys(gi, which, i)
            TT(sqb[:, fi % 4, :], raw, raw, ALU.mult, keys, [("sqb", fi % 4)], eng="pool")

        def qk_norm2(gi, which, i0, fi0):
            seg, b = GB[gi]
            m0 = 2 * b
            sl = (fi0 // 2) % 2
            tens = qT if which == 0 else kT
            nm = "qT" if which == 0 else "kT"
            raw = tens[:, i0:i0 + 2, s8(m0):s8(m0) + 2, :].rearrange("p i a b -> p i (a b)")
            keys = [(nm, i, s8(m0 + t)) for i in (i0, i0 + 1) for t in (0, 1)]
            rb_ = nb()
            for t in range(2):
                sq = (fi0 + t) % 4
                MM(bank(rb_, t * 256, t * 256 + 256), blk[:], sqb[:, sq, :], True, True, [("sqb", sq), "c_blk"], bk(rb_))
            ACT(lnb[:, sl, :], bank(rb_), AF.Ln, bk(rb_) + ["cst"], [("lnb", sl)], bias=cst[:, 0:1], scale=1.0 / 64)
            bcol = 1 if which == 0 else 2
            ACT(lnb[:, sl, :], lnb[:, sl, :], AF.Exp, [("lnb", sl), "cst"], [("lnb", sl)], bias=cst[:, bcol:bcol + 1], scale=-0.5)
            STT(raw, raw, gqk[:, which:which + 1], lnb[:, sl, :].rearrange("p (i c) -> p i c", i=2), ALU.mult, ALU.mult,
                keys + [("lnb", sl), "c_gqk"], keys)

        def pool_tile(seg, m):
            nt = seg["nt"]
            if seg["kind"] == "prompt":
                mk_ = "first" if m == 0 else ("last" if m == nt - 1 else "main")
            else:
                mk_ = "sfirst" if m == seg["qlo"] else ("slast" if m == seg["qhi"] - 1 else "main")
            parts = [(m, mk_)]
            if m - 1 >= 0:
                parts.append((m - 1, "prev"))
            if m + 1 < nt:
                parts.append((m + 1, "next"))
            r1 = nb()
            for g in range(4):
                for pi, (mm, kd) in enumerate(parts):
                    c0, c1 = (0, 8) if kd == "prev" else ((120, 128) if kd == "next" else (0, 128))
                    MM(bank(r1, g * 128 + c0, g * 128 + c1), utm[:, s4(mm), g * 128:(g + 1) * 128],
                       bandb[:, g * 7 + BK[kd], c0:c1],
                       pi == 0, pi == len(parts) - 1, [("utm", s4(mm)), "c_bands"], bk(r1))
            ACT(pooled[:, m % 2].rearrange("p g c -> p (g c)"), bank(r1), AF.Copy, bk(r1), [("pooled", m % 2)])

        def pool_tile2(seg, m):
            r2 = nb()
            for g in range(4):
                MM(bank(r2, g * 128, (g + 1) * 128), wpb[:, g, :], pooled[:, m % 2, g, :], True, True,
                   [("pooled", m % 2), "wpb"], bk(r2))
            for g in range(4):
                STT(mx[:, g, s8(m), :], bank(r2, g * 128, (g + 1) * 128), psc[:, g:g + 1], sgp[:, g, s4(m), :],
                    ALU.mult, ALU.mult, bk(r2) + [("sgp", g, s4(m)), "c_psc"], [("mx", g, s8(m))])

        def stageB_rest(gi):
            seg, b = GB[gi]
            par = gi % 2
            m0 = 2 * b
            has_q = isq(seg, m0) or isq(seg, m0 + 1)
            fm = ([("q", 1024 + 128 * i, i) for i in range(4)] if has_q else []) + [("k", 1536 + 128 * i, i) for i in range(4)]

            def need_u(m):
                return any(isq(seg, mm) for mm in (m - 1, m, m + 1))

            def tm(ti, which):
                m = m0 + ti
                col0 = 0 if which == 0 else 2048
                rb_ = nb()
                for c in range(8):
                    MM(bank(rb_), hT[:, par, c, ti * 128:(ti + 1) * 128], wib[:, c, col0:col0 + 512], c == 0, c == 7,
                       [("hT", par, ti)] + WIB, bk(rb_))
                if which == 0:
                    CP(utm[:, s4(m), :], bank(rb_), bk(rb_), [("utm", s4(m))])
                else:
                    ACT(vr[:, s8(m), :], bank(rb_), AF.Copy, bk(rb_), [("vr", s8(m))])

            if need_u(m0):
                tm(0, 0)
                yield
            if need_u(m0 + 1):
                tm(1, 0)
                yield
            p2q = []
            ptiles = [m for m in (m0 - 1, m0) if m >= 0 and isq(seg, m)]
            if m0 + 2 == seg["nt"] and isq(seg, m0 + 1):
                ptiles.append(m0 + 1)
            for fi, (kind, col0, i) in enumerate(fm):
                fm_chunk(gi, fi, kind, col0, i)
                yield
                if fi == 3:
                    tm(0, 1)
                    yield
                if p2q and fi in (2, 5):
                    pool_tile2(seg, p2q.pop(0))
                    yield
                if fi in (1, 4) and ptiles:
                    mp = ptiles.pop(0)
                    pool_tile(seg, mp)
                    p2q.append(mp)
                    yield
                qk_sq(gi, 0 if kind == "q" else 1, i, fi)
                if fi >= 3 and fi % 2 == 1:
                    kind2, _, i2 = fm[fi - 3]
                    qk_norm2(gi, 0 if kind2 == "q" else 1, i2, fi - 3)
            nf = len(fm)
            yield
            tm(1, 1)
            kind2, _, i2 = fm[nf - 2]
            qk_norm2(gi, 0 if kind2 == "q" else 1, i2, nf - 2)
            yield
            while ptiles or p2q:
                if p2q:
                    pool_tile2(seg, p2q.pop(0))
                    yield
                    continue
                mp = ptiles.pop(0)
                pool_tile(seg, mp)
                p2q.append(mp)
                yield
                continue
                yield

        it_ctr = [0]

        def stageC(gi):
            seg, b = GB[gi]
            pend = None
            for j in (2 * b, 2 * b + 1):
                if not isq(seg, j):
                    continue
                variants = attn_variants(seg, j)
                jp = j % 2
                if any(v[0] == "reg" for v in variants):
                    ACT(kmrg[:, jp, :, 0:64], kT[:, :, s8(j + 2), 0:64], AF.Copy, [("kT", hp, s8(j + 2)) for hp in range(4)],
                        [("kmrg", jp)])
                    ACT(kmrg[:, jp, :, 64:128], kT[:, :, s8(j - 2), 64:128], AF.Copy, [("kT", hp, s8(j - 2)) for hp in range(4)],
                        [("kmrg", jp)])
                    ACT(vmrg[0:64, jp, :], vr[0:64, s8(j + 2), :], AF.Copy, [("vr", s8(j + 2))], [("vmrg", jp)])
                    ACT(vmrg[64:128, jp, :], vr[64:128, s8(j - 2), :], AF.Copy, [("vr", s8(j - 2))], [("vmrg", jp)])
                for hp in range(4):
                    for vi, var in enumerate(variants):
                        st = it_ctr[0] % 2
                        it_ctr[0] += 1
                        fl = var[-1]
                        hA, hB = 2 * hp, 2 * hp + 1
                        qk = ("qT", hp, s8(j))
                        if var[0] == "reg":
                            e0 = 0
                            blocks = [(("kT", j + 1), 0, 128, 0), (("kT", j), 0, 128, 128), (("kT", j - 1), 0, 128, 256),
                                      (("kT", j - 2), 0, 64, 384), (("mrg", jp), 64, 128, 448)]
                        else:
                            kb_hi = var[1]
                            e0 = NT_REG + 6 - 2 * (kb_hi - j)
                            blocks = [(("kT", kb_hi - i), 0, 128, 128 * i) for i in range(4)]
                        for (src, ql, qh, sc) in blocks:
                            n = qh - ql
                            for hh in range(2):
                                r0 = 64 * hh
                                if src[0] == "kT":
                                    lhs = kT[r0:r0 + 64, hp, s8(src[1]), :]
                                    rk = ("kT", hp, s8(src[1]))
                                else:
                                    lhs = kmrg[r0:r0 + 64, src[1], hp, :]
                                    rk = ("kmrg", src[1])
                                MM(bank(3 + hh, sc, sc + n), lhs, qT[r0:r0 + 64, hp, s8(j), ql:qh], True, True,
                                   [rk, qk], bk(3 + hh))
                        slots = [2 * st, 2 * st + 1]
                        pk = [("pet", slots[0]), ("pet", slots[1])]
                        tk = [("ptt", slots[0]), ("ptt", slots[1])]
                        ACT(pet[:, 2 * st:2 * st + 2, :].rearrange("p a b -> p (a b)"), pm[:, 3 * 512:5 * 512], AF.Exp,
                            bk(3) + bk(4), pk)
                        TT(ptt[:, 2 * st:2 * st + 2, :], pet[:, 2 * st:2 * st + 2, :],
                           etab[:, hA:hA + 2, e0:e0 + 8, :].rearrange("p h t c -> p h (t c)"), ALU.mult,
                           pk + ETAB, tk)
                        yield
                        if pend is not None:
                            pend()
                            yield

                        def pv(st=st, slots=slots, blocks=blocks, hA=hA, hB=hB, fl=fl, vi=vi, hp=hp, j=j):
                            ndb = 5 + st
                            for grp in range(2):
                                ob = ndb * 512 + grp * 128
                                for ii, (src, ql, qh, sc) in enumerate(blocks):
                                    n = qh - ql
                                    for hh, hd in enumerate([hA, hB]):
                                        po = 64 * hh
                                        if grp == 1:
                                            lhs, rds = ones64[:], ["c_ones"]
                                        elif src[0] == "kT":
                                            lhs, rds = vr[:, s8(src[1]), hd * 64:(hd + 1) * 64], [("vr", s8(src[1]))]
                                        else:
                                            lhs, rds = vmrg[:, src[1], hd * 64:(hd + 1) * 64], [("vmrg", src[1])]
                                        MM(pm[po:po + 64, ob + ql:ob + qh], lhs, ptt[:, slots[hh], sc:sc + n],
                                           ii == 0, ii == len(blocks) - 1, rds + [("ptt", slots[hh])], bk(ndb))
                            num = bank(ndb, 0, 128)
                            den = bank(ndb, 128, 256)
                            ACT(rec[:, st, :], den, AF.Ln, bk(ndb), [("rec", st)])
                            ACT(rec[:, st, :], rec[:, st, :], AF.Exp, [("rec", st)], [("rec", st)], scale=-1.0)
                            if fl is None:
                                TT(tf1[:, st, :], num, rec[:, st, :], ALU.mult, bk(ndb) + [("rec", st)], [("tf1", st)])
                            else:
                                STT(tf1[:, st, :], num, flags[:, fl:fl + 1], rec[:, st, :], ALU.mult, ALU.mult,
                                    bk(ndb) + [("rec", st), "c_flags"], [("tf1", st)])
                            mk_ = ("mx", 4 + hp, s8(j))
                            if vi == 0:
                                TT(mx[:, 4 + hp, s8(j), :], tf1[:, st, :], sg[:, hp, s8(j), :], ALU.mult,
                                   [("tf1", st), ("sg", hp, s8(j))], [mk_])
                            else:
                                TT(tf2[:], tf1[:, st, :], sg[:, hp, s8(j), :], ALU.mult,
                                   [("tf1", st), ("sg", hp, s8(j))], ["tf2"])
                                TT(mx[:, 4 + hp, s8(j), :], mx[:, 4 + hp, s8(j), :], tf2[:], ALU.add, ["tf2", mk_], [mk_])
                        pend = pv
            if pend is not None:
                pend()
                yield

        def stageD(gi):
            seg, b = GB[gi]
            for j in (2 * b, 2 * b + 1):
                if not isq(seg, j):
                    continue
                sl = j % 2
                xrow = (seg["x0"] + j) * 128
                yrow = (seg["y0"] + j - seg["qlo"]) * 128
                xk = "xr%d" % sl
                for half in range(2):
                    rb_ = nb()
                    for c in range(8):
                        MM(bank(rb_), mx[:, c, s8(j), :], wob[:, c, half * 512:(half + 1) * 512], c == 0, c == 7,
                           [("mx", c, s8(j))] + WOB, bk(rb_))
                    TT(xr[:, sl, half * 512:(half + 1) * 512], bank(rb_), xr[:, sl, half * 512:(half + 1) * 512], ALU.add,
                       bk(rb_) + [xk], [xk])
                    yield
                DMA(yout[yrow:yrow + 128, :], xr[:, sl, :], reads=[xk], key="yo%d" % sl, final=True)

        def drain(g):
            for _ in g:
                pass

        def merge(gens):
            gens = [[g, 0, max(u, 1)] for g, u in gens]
            while gens:
                gens.sort(key=lambda t: (t[1] + 1) / t[2])
                t = gens[0]
                try:
                    next(t[0])
                    t[1] += 1
                except StopIteration:
                    gens.remove(t)

        NG = len(GB)
        import os
        NIT = int(os.environ.get("DBG_NIT", NG + 3))
        STG = os.environ.get("DBG_STAGES", "AGBCD")
        stageA0(0)
        drain(stageA(0))
        stageA0(1)
        for k in range(0, min(NG + 3, NIT)):
            if 0 <= k - 3 < NG:
                stageD0(k - 3)
            if k < NG and "G" in STG:
                drain(stageB_gates(k))
            gl = []
            if k < NG and "B" in STG:
                gl.append((stageB_rest(k), 16))
            if 0 <= k - 2 < NG and "C" in STG:
                gl.append((stageC(k - 2), int(os.environ.get('K_CW', 14))))
            if 0 <= k - 3 < NG and "D" in STG:
                gl.append((stageD(k - 3), 4))
            if k + 1 < NG and "A" in STG:
                gl.append((stageA(k + 1), 4))
            merge(gl)
            if k + 2 < NG:
                stageA0(k + 2)
            if k == 0:
                setup_part2()

        P.emit(block, sems, dsem)
    return nc


_NC_CACHE = {}


def kernel(x_prompt, x_sample, norm_g, w_in, w_pool, pool_scale, q_norm_g, k_norm_g, rpb, w_out):
    x_prompt = np.asarray(x_prompt, np.float32)
    x_sample = np.asarray(x_sample, np.float32)
    if "nc" not in _NC_CACHE:
        _NC_CACHE["nc"] = build_program()
    nc = _NC_CACHE["nc"]
    biasg, maskg = build_bias_mask(np.asarray(rpb, np.float32)[0])
    ident = np.eye(128, dtype=np.float32).astype(bf)
    blk = np.kron(np.eye(2, dtype=np.float32), np.ones((64, 64), np.float32)).astype(bf)
    ones64 = np.ones((128, 64), np.float32).astype(bf)
    ng = np.ascontiguousarray(np.asarray(norm_g, np.float32)[0].reshape(8, 128).T)
    psc = np.ascontiguousarray(np.asarray(pool_scale, np.float32)[0].reshape(4, 128).T)
    gq = np.asarray(q_norm_g, np.float32)[0]
    gk = np.asarray(k_norm_g, np.float32)[0]
    gqk = np.ascontiguousarray(np.stack([np.concatenate([gq, gq]), np.concatenate([gk, gk])], axis=1))
    w_in0 = np.ascontiguousarray(np.asarray(w_in, np.float32)[0])
    w_out0 = np.ascontiguousarray(np.asarray(w_out, np.float32)[0])
    w_pool0 = np.ascontiguousarray(np.asarray(w_pool, np.float32)[0].reshape(512, 128))
    in_maps = []
    for c in range(NCORES):
        sb, sq = c // 4, c % 4
        r0, r1 = sq * 2048 - 256, sq * 2048 + 2048 + 256
        piece = np.zeros((2560, D), np.float32)
        a0, a1 = max(r0, 0), min(r1, 8192)
        piece[a0 - r0:a1 - r0] = x_sample[sb, a0:a1]
        xin = np.concatenate([x_prompt[2 * c], x_prompt[2 * c + 1], piece], axis=0)
        top, bot = (sq == 0), (sq == 3)
        fl = np.tile(np.array([[float(top), 1.0 - float(top), float(bot), 1.0 - float(bot)]], np.float32), (128, 1))
        bands = build_bands(top, bot).reshape(128, NB * 128).astype(bf)
        in_maps.append(dict(xin=np.ascontiguousarray(xin), w_in=w_in0, w_out=w_out0, w_pool=w_pool0, ng=ng, psc=psc,
                            gqk=gqk, biasg=biasg, maskg=maskg.astype(bf), bands=bands, ident=ident, blk=blk, ones64=ones64,
                            flags=fl))
    res = run_bass_kernel_spmd(nc, in_maps, core_ids=list(range(NCORES)))
    y_prompt = np.empty((16, 4096, D), np.float32)
    y_sample = np.empty((2, 8192, D), np.float32)
    for c in range(NCORES):
        y = np.asarray(res.results[c]["yout"])
        y_prompt[2 * c] = y[0:4096]
        y_prompt[2 * c + 1] = y[4096:8192]
        y_sample[c // 4, (c % 4) * 2048:(c % 4 + 1) * 2048] = y[8192:10240]
    return (y_prompt, y_sample)
```

```python
import os
import numpy as np
import ml_dtypes
from contextlib import ExitStack
import concourse.bass as bass
import concourse.mybir as mybir
from concourse.bass_utils import run_bass_kernel_spmd

F32 = mybir.dt.float32
BF16 = mybir.dt.bfloat16
AF = mybir.ActivationFunctionType
ALU = mybir.AluOpType
bf = ml_dtypes.bfloat16

NCORES = 8
D = 1024
EPS = 1e-6
NT_REG, NT_ALL = 8, 14
NTL = NT_REG + NT_ALL
POOL_W = (2, 4, 8, 16)
BK = {"main": 0, "prev": 1, "next": 2, "first": 3, "last": 4, "sfirst": 5, "slast": 6}
NB = 28

COMPUTE = ("pe", "act", "dve", "pool")


class Op:
    __slots__ = ("eng", "fn", "deps", "idx", "need_inc", "inc_no", "dma_key", "dma_cnt", "is_dma")

    def __init__(self, eng, fn):
        self.eng = eng
        self.fn = fn
        self.deps = []
        self.need_inc = False
        self.inc_no = 0
        self.is_dma = False
        self.dma_key = None
        self.dma_cnt = 0


class Prog:
    def __init__(self, same_engine_sync=True):
        self.ops = {e: [] for e in COMPUTE + ("sp", "gq")}
        self.last_w = {}
        self.readers = {}
        self.dma_cnt = {}
        self.same_engine_sync = same_engine_sync
        self.final_dma = []

    def _dep(self, op, other):
        if other is None or other is op:
            return
        if other.eng == op.eng and not other.is_dma:
            if op.eng == "pe" or not self.same_engine_sync:
                return
        if other not in op.deps:
            op.deps.append(other)
            if not other.is_dma:
                other.need_inc = True

    def add(self, eng, fn, reads=(), writes=(), dma_key=None, final=False):
        op = Op(eng, fn)
        if dma_key is not None:
            op.is_dma = True
            op.dma_key = dma_key
            self.dma_cnt[dma_key] = self.dma_cnt.get(dma_key, 0) + 16
            op.dma_cnt = self.dma_cnt[dma_key]
            if final:
                self.final_dma.append(op)
        for r in reads:
            self._dep(op, self.last_w.get(r))
        for w in writes:
            self._dep(op, self.last_w.get(w))
            for rd in self.readers.get(w, ()):
                self._dep(op, rd)
        for r in reads:
            self.readers.setdefault(r, []).append(op)
        for w in writes:
            self.last_w[w] = op
            self.readers[w] = []
        self.ops[eng].append(op)
        return op

    def emit(self, block, sems, dma_sems):
        for e in COMPUTE:
            n = 0
            for op in self.ops[e]:
                if op.need_inc:
                    n += 1
                    op.inc_no = n

        def run(eng_name, h):
            waited = {}
            for op in self.ops[eng_name]:
                for d in op.deps:
                    if d.is_dma:
                        s, v = dma_sems[d.dma_key], d.dma_cnt
                    else:
                        s, v = sems[d.eng], d.inc_no
                    k = id(s)
                    if waited.get(k, 0) < v:
                        h.wait_ge(s, v)
                        waited[k] = v
                ins = op.fn(h)
                if op.is_dma:
                    ins.then_inc(dma_sems[op.dma_key], 16)
                elif op.need_inc:
                    ins.then_inc(sems[op.eng], 1)
            if eng_name == "sp":
                for op in self.final_dma:
                    s, v = dma_sems[op.dma_key], op.dma_cnt
                    if waited.get(id(s), 0) < v:
                        h.wait_ge(s, v)
                        waited[id(s)] = v

        @block.tensor
        def _(h):
            run("pe", h)

        @block.scalar
        def _(h):
            run("act", h)

        @block.vector
        def _(h):
            run("dve", h)

        @block.gpsimd
        def _(h):
            run("pool", h)

        @block.sync
        def _(h):
            run("sp", h)


def tile_specs():
    reg = [(d, d + 1) for d in (2, 1, 0, -1, -2, -3, -4)] + [(3, -4)]
    al = [(d if abs(d) <= 7 else None, d + 1 if abs(d + 1) <= 7 else None) for d in range(6, -8, -1)]
    return reg + al


def build_bias_mask(rpb):
    kc = np.arange(64)[:, None]
    qc = np.arange(64)[None, :]
    cs = np.clip(qc - 8, 0, 48)
    colmask = ((kc >= cs) & (kc < cs + 16)).astype(np.float32)
    dc = np.clip(kc - qc + 15, 0, 30)
    biasg = np.zeros((8, 128, NTL * 64), np.float32)
    maskg = np.zeros((128, NTL * 64), np.float32)
    for ti, drs in enumerate(tile_specs()):
        for kr in range(2):
            dr = drs[kr]
            if dr is None:
                continue
            biasg[:, kr * 64:(kr + 1) * 64, ti * 64:(ti + 1) * 64] = rpb[:, dr + 7][:, dc]
            maskg[kr * 64:(kr + 1) * 64, ti * 64:(ti + 1) * 64] = colmask
    return biasg, maskg


def build_bands(top_edge, bot_edge):
    bands = np.zeros((128, NB, 128), np.float32)
    t = np.arange(128)[:, None]
    tp = np.arange(128)[None, :]
    for g, w in enumerate(POOL_W):
        a = w // 2
        b = w - a
        lo, hi = tp - a, tp + b
        eye = (t == tp).astype(np.float32)
        main = ((t >= lo) & (t < hi)).astype(np.float32) / w - eye
        prev = ((t - 128 >= lo) & (t - 128 < hi)).astype(np.float32) / w
        nxt = ((t + 128 >= lo) & (t + 128 < hi)).astype(np.float32) / w
        cnt_f = (hi - np.maximum(lo, 0)).astype(np.float32)
        first = ((t >= lo) & (t < hi)).astype(np.float32) / cnt_f - eye
        cnt_l = (np.minimum(hi, 128) - lo).astype(np.float32)
        last = ((t >= lo) & (t < hi)).astype(np.float32) / cnt_l - eye
        bands[:, g * 7 + BK["main"]] = main
        bands[:, g * 7 + BK["prev"]] = prev
        bands[:, g * 7 + BK["next"]] = nxt
        bands[:, g * 7 + BK["first"]] = first
        bands[:, g * 7 + BK["last"]] = last
        bands[:, g * 7 + BK["sfirst"]] = first if top_edge else main
        bands[:, g * 7 + BK["slast"]] = last if bot_edge else main
    return bands


def segments():
    return [
        dict(x0=0, nt=32, qlo=0, qhi=32, y0=0, kind="prompt"),
        dict(x0=32, nt=32, qlo=0, qhi=32, y0=32, kind="prompt"),
        dict(x0=64, nt=20, qlo=2, qhi=18, y0=64, kind="sample"),
    ]


def attn_variants(seg, j):
    if seg["kind"] == "prompt":
        last = seg["nt"] - 1
        if j <= 1:
            return [("al", 3, None)]
        if j >= last - 1:
            return [("al", last, None)]
        return [("reg", None)]
    lo, hi = seg["qlo"], seg["qhi"] - 1
    if j <= lo + 1:
        return [("reg", 1), ("al", lo + 3, 0)]
    if j >= hi - 1:
        return [("reg", 3), ("al", hi, 2)]
    return [("reg", None)]


def build_program():
    nc = bass.Bass("TRN2", target_bir_lowering=False)
    segs = segments()
    NXT = sum(s["nt"] for s in segs)
    NYT = sum(s["qhi"] - s["qlo"] for s in segs)

    def din(name, shape, dt):
        return nc.dram_tensor(name, shape, dt, kind="ExternalInput").ap()

    xin = din("xin", [NXT * 128, D], F32)
    w_in_d = din("w_in", [D, 3072], F32)
    w_out_d = din("w_out", [D, D], F32)
    w_pool_d = din("w_pool", [4 * 128, 128], F32)
    ng_d = din("ng", [128, 8], F32)
    psc_d = din("psc", [128, 4], F32)
    gqk_d = din("gqk", [128, 2], F32)
    biasg_d = din("biasg", [8, 128, NTL * 64], F32)
    maskg_d = din("maskg", [128, NTL * 64], BF16)
    bands_d = din("bands", [128, NB * 128], BF16)
    ident_d = din("ident", [128, 128], BF16)
    blk_d = din("blk", [128, 128], BF16)
    ones_d = din("ones64", [128, 64], BF16)
    flags_d = din("flags", [128, 4], F32)
    yout = nc.dram_tensor("yout", [NYT * 128, D], F32, kind="ExternalOutput").ap()

    P = Prog()
    with ExitStack() as ES:
        def SB(name, shape, dt):
            return ES.enter_context(nc.sbuf_tensor(name, shape, dt))

        def SEM(name):
            return ES.enter_context(nc.semaphore(name))

        wib = SB("wib", [128, 8, 3072], BF16)
        wob = SB("wob", [128, 8, 1024], BF16)
        wpb = SB("wpb", [128, 4, 128], BF16)
        etab = SB("etab", [128, 8, NTL, 64], BF16)
        bandb = SB("bandb", [128, NB, 128], BF16)
        ident = SB("ident_s", [128, 128], BF16)
        blk = SB("blk_s", [128, 128], BF16)
        ones64 = SB("ones_s", [128, 64], BF16)
        ng = SB("ng_s", [128, 8], F32)
        psc = SB("psc_s", [128, 4], F32)
        gqk = SB("gqk_s", [128, 2], F32)
        flags = SB("flags_s", [128, 4], F32)
        cst = SB("cst", [128, 4], F32)
        kT = SB("kT", [128, 4, 8, 128], BF16)
        vr = SB("vr", [128, 8, 512], BF16)
        qT = SB("qT", [128, 4, 8, 128], BF16)
        sg = SB("sg", [128, 4, 8, 128], BF16)
        sgp = SB("sgp", [128, 4, 4, 128], BF16)
        mx = SB("mx", [128, 8, 8, 128], BF16)
        utm = SB("utm", [128, 4, 512], BF16)
        hT = SB("hT", [128, 2, 8, 256], BF16)
        xt = SB("xt", [128, 2, 1024], F32)
        xs = SB("xs", [128, 2, 1024], BF16)
        xr = SB("xr", [128, 2, 1024], F32)
        pooled = SB("pooled", [128, 2, 4, 128], BF16)
        sqb = SB("sqb", [128, 4, 256], BF16)
        lnb = SB("lnb", [128, 2, 512], F32)
        kmrg = SB("kmrg", [128, 2, 4, 128], BF16)
        vmrg = SB("vmrg", [128, 2, 512], BF16)
        pet = SB("pet", [128, 4, 512], BF16)
        ptt = SB("ptt", [128, 4, 512], BF16)
        rec = SB("rec", [128, 2, 128], F32)
        tf1 = SB("tf1", [128, 2, 128], F32)
        tf2 = SB("tf2", [128, 128], F32)
        ssq = SB("ssq", [128, 2], F32)
        lsq = SB("lsq", [128, 2], F32)
        rstd = SB("rstd", [128, 2], F32)
        pT = ES.enter_context(nc.psum_tensor("pT", [128, 1024], BF16))
        pm = ES.enter_context(nc.psum_tensor("pm", [128, 7 * 512], F32))

        sems = {e: SEM("s_" + e) for e in COMPUTE}
        dkeys = ["c_ident", "c_blk", "c_ones", "c_ng", "c_psc", "c_gqk", "c_flags", "c_bands", "c_mask",
                 "xt0", "xt1", "xr0", "xr1", "yo0", "yo1"]
        dsem = {k: SEM("d_" + k) for k in dkeys}
        block = ES.enter_context(nc.Block())

        def bank(i, lo=0, hi=512):
            return pm[:, i * 512 + lo:i * 512 + hi]

        def bk(i):
            return [("ps", i, 0), ("ps", i, 1)]

        rot = [0]

        def nb():
            r = rot[0] % 3
            rot[0] += 1
            return r

        def MM(out, lhsT, rhs, start, stop, reads, writes):
            P.add("pe", lambda h: h.matmul(out, lhsT=lhsT, rhs=rhs, start=start, stop=stop), reads=reads, writes=writes)

        def TR(out, in_, reads, writes):
            P.add("pe", lambda h: h.transpose(out=out, in_=in_, identity=ident[:]), reads=reads, writes=writes)

        def ACT(out, in_, func, reads, writes, bias=None, scale=None, accum_out=None):
            kw = {}
            if bias is not None:
                kw["bias"] = bias
            if scale is not None:
                kw["scale"] = scale
            if accum_out is not None:
                kw["accum_out"] = accum_out
            P.add("act", lambda h: h.activation(out=out, in_=in_, func=func, **kw), reads=reads, writes=writes)

        def TT(out, in0, in1, op, reads, writes, eng="dve"):
            P.add(eng, lambda h: h.tensor_tensor(out=out, in0=in0, in1=in1, op=op), reads=reads, writes=writes)

        def TS(out, in0, scalar1, op0, reads, writes, eng="dve"):
            P.add(eng, lambda h: h.tensor_scalar(out=out, in0=in0, scalar1=scalar1, scalar2=None, op0=op0),
                  reads=reads, writes=writes)

        def STT(out, in0, scalar, in1, op0, op1, reads, writes):
            P.add("dve", lambda h: h.scalar_tensor_tensor(out=out, in0=in0, scalar=scalar, in1=in1, op0=op0, op1=op1),
                  reads=reads, writes=writes)

        def CP(out, in_, reads, writes, eng="dve"):
            P.add(eng, lambda h: h.tensor_copy(out=out, in_=in_), reads=reads, writes=writes)

        def RCPF(out, in_, reads, writes):
            P.add("dve", lambda h: h.reciprocal_approx_fast(out=out, in_=in_), reads=reads, writes=writes)

        def DMA(out, in_, reads=(), writes=(), key=None, final=False):
            P.add("sp", lambda h: h.dma_start(out=out, in_=in_), reads=reads, writes=writes, dma_key=key, final=final)

        for key, dst, src in [("c_ident", ident, ident_d), ("c_blk", blk, blk_d), ("c_ones", ones64, ones_d),
                              ("c_ng", ng, ng_d), ("c_psc", psc, psc_d), ("c_gqk", gqk, gqk_d),
                              ("c_flags", flags, flags_d)]:
            DMA(dst[:], src, writes=[key], key=key)
        DMA(bandb[:].rearrange("p a b -> p (a b)"), bands_d, writes=["c_bands"], key="c_bands")
        P.add("dve", lambda h: h.memset(cst[:, 0:1], EPS), writes=["cst"])
        P.add("dve", lambda h: h.memset(cst[:, 1:2], -float(np.log(8.0))), writes=["cst"])
        P.add("dve", lambda h: h.memset(cst[:, 2:3], 0.0), writes=["cst"])
        stgs = [(xr[:, 0, :], "xr0"), (xr[:, 1, :], "xr1"), (xt[:, 0, :], "xt0"), (xt[:, 1, :], "xt1")]
        sctr = [0]

        def next_stg():
            r = stgs[sctr[0] % 4]
            sctr[0] += 1
            return r

        ci = 0
        for c in range(8):
            for t3 in range(3):
                stg, key = next_stg()
                DMA(stg, w_in_d[c * 128:(c + 1) * 128, t3 * 1024:(t3 + 1) * 1024], writes=[key], key=key)
                dst = wib[:, c, t3 * 1024:(t3 + 1) * 1024]
                if ci % 2 == 0:
                    TS(dst, stg, ng[:, c:c + 1], ALU.mult, [key, "c_ng"], [("wib", c, t3)])
                else:
                    TS(dst, stg, ng[:, c:c + 1], ALU.mult, [key, "c_ng"], [("wib", c, t3)])
                ci += 1
        stg, key = next_stg()
        DMA(stg[:, 0:512].rearrange("p (g c) -> p g c", g=4), w_pool_d.rearrange("(g p) c -> p g c", p=128),
            writes=[key], key=key)
        CP(wpb[:].rearrange("p g c -> p (g c)"), stg[:, 0:512], [key], ["wpb"])

        def setup_part2():
            NTH = NTL * 64
            hs = NTH // 2
            petf = pet[:].rearrange("p a b -> p (a b)")
            PETK = [("pet", i) for i in range(4)]
            DMA(petf[:, 0:NTH], maskg_d, writes=PETK, key="c_mask")
            i2 = 0
            for h8 in range(8):
                for hf in range(2):
                    sl = i2 % 2
                    i2 += 1
                    key = "xr%d" % sl
                    sb_ = xr[:, sl, 0:hs]
                    DMA(sb_, biasg_d[h8, :, hf * hs:(hf + 1) * hs], writes=[key], key=key)
                    ACT(sb_, sb_, AF.Exp, [key], [key])
                    TT(etab[:, h8].rearrange("p t c -> p (t c)")[:, hf * hs:(hf + 1) * hs], sb_,
                       petf[:, hf * hs:(hf + 1) * hs], ALU.mult, [key] + PETK, [("etab", h8, hf)])
            for c in range(8):
                sl = i2 % 2
                i2 += 1
                key = "xr%d" % sl
                DMA(xr[:, sl, :], w_out_d[c * 128:(c + 1) * 128, :], writes=[key], key=key)
                CP(wob[:, c, :], xr[:, sl, :], [key], [("wob", c)])

        WIB = [("wib", c, t3) for c in range(8) for t3 in range(3)]
        WOB = [("wob", c) for c in range(8)]
        ETAB = [("etab", h8, hf) for h8 in range(8) for hf in range(2)]

        GB = [(seg, b) for seg in segs for b in range(seg["nt"] // 2)]

        def s8(m):
            return m % 8

        def s4(m):
            return m % 4

        def isq(seg, m):
            return seg["qlo"] <= m < seg["qhi"]

        def stageA0(gi):
            seg, b = GB[gi]
            for ti in range(2):
                m = 2 * b + ti
                sl = m % 2
                row0 = (seg["x0"] + m) * 128
                DMA(xt[:, sl, :], xin[row0:row0 + 128, :], writes=["xt%d" % sl], key="xt%d" % sl)

        def stageD0(gi):
            seg, b = GB[gi]
            for j in (2 * b, 2 * b + 1):
                if not isq(seg, j):
                    continue
                sl = j % 2
                xrow = (seg["x0"] + j) * 128
                DMA(xr[:, sl, :], xin[xrow:xrow + 128, :], writes=["xr%d" % sl], key="xr%d" % sl)

        def stageA(gi):
            seg, b = GB[gi]
            par = gi % 2
            for ti in range(2):
                m = 2 * b + ti
                sl = m % 2
                xk = "xt%d" % sl
                ACT(xs[:, sl, :], xt[:, sl, :], AF.Square, [xk], [("xs", sl), ("ssq", sl)], accum_out=ssq[:, sl:sl + 1])
                ACT(lsq[:, sl:sl + 1], ssq[:, sl:sl + 1], AF.Ln, [("ssq", sl), "cst"], [("lsq", sl)],
                    bias=cst[:, 0:1], scale=1.0 / D)
                ACT(rstd[:, sl:sl + 1], lsq[:, sl:sl + 1], AF.Exp, [("lsq", sl)], [("rstd", sl)], scale=-0.5)
                TS(xs[:, sl, :], xt[:, sl, :], rstd[:, sl:sl + 1], ALU.mult, [xk, ("rstd", sl)], [("xs", sl)])
                yield
                for c in range(8):
                    TR(pT[:, c * 128:(c + 1) * 128], xs[:, sl, c * 128:(c + 1) * 128], [("xs", sl), "c_ident"], ["pT"])
                CP(hT[:, par, :, ti * 128:(ti + 1) * 128], pT[:].rearrange("p (c t) -> p c t", c=8), ["pT"],
                   [("hT", par, ti)])
                yield

        def fm_chunk(gi, fi, kind, col0, i):
            seg, b = GB[gi]
            par = gi % 2
            m0 = 2 * b
            rb_ = nb()
            o = bank(rb_, 0, 256)
            ok = bk(rb_)
            for c in range(8):
                MM(o, wib[:, c, col0:col0 + 128], hT[:, par, c, :], c == 0, c == 7,
                   [("hT", par, 0), ("hT", par, 1)] + WIB, ok)
            if kind == "gp":
                ACT(sgp[:, i, s4(m0):s4(m0) + 2, :].rearrange("p a b -> p (a b)"), o, AF.Silu, ok,
                    [("sgp", i, s4(m0)), ("sgp", i, s4(m0 + 1))])
            elif kind == "ga":
                ACT(sg[:, i, s8(m0):s8(m0) + 2, :].rearrange("p a b -> p (a b)"), o, AF.Silu, ok,
                    [("sg", i, s8(m0)), ("sg", i, s8(m0 + 1))])
            elif kind == "q":
                CP(qT[:, i, s8(m0):s8(m0) + 2, :].rearrange("p a b -> p (a b)"), o, ok,
                   [("qT", i, s8(m0)), ("qT", i, s8(m0 + 1))])
            else:
                ACT(kT[:, i, s8(m0):s8(m0) + 2, :].rearrange("p a b -> p (a b)"), o, AF.Copy, ok,
                    [("kT", i, s8(m0)), ("kT", i, s8(m0 + 1))])

        def stageB_gates(gi):
            seg, b = GB[gi]
            if not (isq(seg, 2 * b) or isq(seg, 2 * b + 1)):
                return
            par = gi % 2
            m0 = 2 * b
            for kind, cbase in (("gp", 512), ("ga", 2560)):
                for i0 in (0, 2):
                    rb_ = nb()
                    for t in range(2):
                        col0 = cbase + 128 * (i0 + t)
                        for c in range(8):
                            MM(bank(rb_, t * 256, t * 256 + 256), wib[:, c, col0:col0 + 128], hT[:, par, c, :],
                               c == 0, c == 7, [("hT", par, 0), ("hT", par, 1)] + WIB, bk(rb_))
                        yield
                    src = bank(rb_).rearrange("p (i c) -> p i c", i=2)
                    if kind == "gp":
                        dst = sgp[:, i0:i0 + 2, s4(m0):s4(m0) + 2, :].rearrange("p i a b -> p i (a b)")
                        keys = [("sgp", i, s4(m0 + t)) for i in (i0, i0 + 1) for t in (0, 1)]
                    else:
                        dst = sg[:, i0:i0 + 2, s8(m0):s8(m0) + 2, :].rearrange("p i a b -> p i (a b)")
                        keys = [("sg", i, s8(m0 + t)) for i in (i0, i0 + 1) for t in (0, 1)]
                    ACT(dst, src, AF.Silu, bk(rb_), keys)

        def qk_keys(gi, which, i):
            seg, b = GB[gi]
            m0 = 2 * b
            if which == 0:
                raw = qT[:, i, s8(m0):s8(m0) + 2, :].rearrange("p a b -> p (a b)")
                keys = [("qT", i, s8(m0)), ("qT", i, s8(m0 + 1))]
            else:
                raw = kT[:, i, s8(m0):s8(m0) + 2, :].rearrange("p a b -> p (a b)")
                keys = [("kT", i, s8(m0)), ("kT", i, s8(m0 + 1))]
            return raw, keys

        def qk_sq(gi, which, i, fi):
            raw, keys = qk_keys(gi, which, i)
            TT(sqb[:, fi % 4, :], raw, raw, ALU.mult, keys, [("sqb", fi % 4)], eng="pool")

        def qk_norm2(gi, which, i0, fi0):
            seg, b = GB[gi]
            m0 = 2 * b
            sl = (fi0 // 2) % 2
            tens = qT if which == 0 else kT
            nm = "qT" if which == 0 else "kT"
            raw = tens[:, i0:i0 + 2, s8(m0):s8(m0) + 2, :].rearrange("p i a b -> p i (a b)")
            keys = [(nm, i, s8(m0 + t)) for i in (i0, i0 + 1) for t in (0, 1)]
            rb_ = nb()
            for t in range(2):
                sq = (fi0 + t) % 4
                MM(bank(rb_, t * 256, t * 256 + 256), blk[:], sqb[:, sq, :], True, True, [("sqb", sq), "c_blk"], bk(rb_))
            ACT(lnb[:, sl, :], bank(rb_), AF.Ln, bk(rb_) + ["cst"], [("lnb", sl)], bias=cst[:, 0:1], scale=1.0 / 64)
            bcol = 1 if which == 0 else 2
            ACT(lnb[:, sl, :], lnb[:, sl, :], AF.Exp, [("lnb", sl), "cst"], [("lnb", sl)], bias=cst[:, bcol:bcol + 1], scale=-0.5)
            STT(raw, raw, gqk[:, which:which + 1], lnb[:, sl, :].rearrange("p (i c) -> p i c", i=2), ALU.mult, ALU.mult,
                keys + [("lnb", sl), "c_gqk"], keys)

        def pool_tile(seg, m):
            nt = seg["nt"]
            if seg["kind"] == "prompt":
                mk_ = "first" if m == 0 else ("last" if m == nt - 1 else "main")
            else:
                mk_ = "sfirst" if m == seg["qlo"] else ("slast" if m == seg["qhi"] - 1 else "main")
            parts = [(m, mk_)]
            if m - 1 >= 0:
                parts.append((m - 1, "prev"))
            if m + 1 < nt:
                parts.append((m + 1, "next"))
            r1 = nb()
            for g in range(4):
                for pi, (mm, kd) in enumerate(parts):
                    c0, c1 = (0, 8) if kd == "prev" else ((120, 128) if kd == "next" else (0, 128))
                    MM(bank(r1, g * 128 + c0, g * 128 + c1), utm[:, s4(mm), g * 128:(g + 1) * 128],
                       bandb[:, g * 7 + BK[kd], c0:c1],
                       pi == 0, pi == len(parts) - 1, [("utm", s4(mm)), "c_bands"], bk(r1))
            ACT(pooled[:, m % 2].rearrange("p g c -> p (g c)"), bank(r1), AF.Copy, bk(r1), [("pooled", m % 2)])

        def pool_tile2(seg, m):
            r2 = nb()
            for g in range(4):
                MM(bank(r2, g * 128, (g + 1) * 128), wpb[:, g, :], pooled[:, m % 2, g, :], True, True,
                   [("pooled", m % 2), "wpb"], bk(r2))
            for g in range(4):
                STT(mx[:, g, s8(m), :], bank(r2, g * 128, (g + 1) * 128), psc[:, g:g + 1], sgp[:, g, s4(m), :],
                    ALU.mult, ALU.mult, bk(r2) + [("sgp", g, s4(m)), "c_psc"], [("mx", g, s8(m))])

        def fm_qk(gi, fi, kind, col0, i, st):
            seg, b = GB[gi]
            par = gi % 2
            m0 = 2 * b
            t = fi % 2
            if t == 0:
                st["rb"] = nb()
            rb_ = st["rb"]
            for c in range(8):
                MM(bank(rb_, t * 256, t * 256 + 256), wib[:, c, col0:col0 + 128], hT[:, par, c, :], c == 0, c == 7,
                   [("hT", par, 0), ("hT", par, 1)] + WIB, bk(rb_))
            if t == 1:
                i0 = i - 1
                tens, nm = (qT, "qT") if kind == "q" else (kT, "kT")
                dst = tens[:, i0:i0 + 2, s8(m0):s8(m0) + 2, :].rearrange("p i a b -> p i (a b)")
                keys = [(nm, ii, s8(m0 + tt)) for ii in (i0, i0 + 1) for tt in (0, 1)]
                src = bank(rb_).rearrange("p (i c) -> p i c", i=2)
                if kind == "q":
                    CP(dst, src, bk(rb_), keys)
                else:
                    ACT(dst, src, AF.Copy, bk(rb_), keys)
                s0 = (fi - 1) % 4
                TT(sqb[:, s0:s0 + 2, :], dst, dst, ALU.mult, keys, [("sqb", s0), ("sqb", s0 + 1)], eng="pool")

        def stageB_rest(gi):
            seg, b = GB[gi]
            par = gi % 2
            m0 = 2 * b
            has_q = isq(seg, m0) or isq(seg, m0 + 1)
            fm = ([("q", 1024 + 128 * i, i) for i in range(4)] if has_q else []) + [("k", 1536 + 128 * i, i) for i in range(4)]

            def need_u(m):
                return any(isq(seg, mm) for mm in (m - 1, m, m + 1))

            def tm(ti, which):
                m = m0 + ti
                col0 = 0 if which == 0 else 2048
                rb_ = nb()
                for c in range(8):
                    MM(bank(rb_), hT[:, par, c, ti * 128:(ti + 1) * 128], wib[:, c, col0:col0 + 512], c == 0, c == 7,
                       [("hT", par, ti)] + WIB, bk(rb_))
                if which == 0:
                    CP(utm[:, s4(m), :], bank(rb_), bk(rb_), [("utm", s4(m))])
                else:
                    ACT(vr[:, s8(m), :], bank(rb_), AF.Copy, bk(rb_), [("vr", s8(m))])

            if need_u(m0):
                tm(0, 0)
                yield
            if need_u(m0 + 1):
                tm(1, 0)
                yield
            p2q = []
            ptiles = [m for m in (m0 - 1, m0) if m >= 0 and isq(seg, m)]
            if m0 + 2 == seg["nt"] and isq(seg, m0 + 1):
                ptiles.append(m0 + 1)
            pst = {}
            for fi, (kind, col0, i) in enumerate(fm):
                fm_qk(gi, fi, kind, col0, i, pst)
                yield
                if fi == 3:
                    tm(0, 1)
                    yield
                if p2q and fi in (2, 5):
                    pool_tile2(seg, p2q.pop(0))
                    yield
                if fi in (1, 4) and ptiles:
                    mp = ptiles.pop(0)
                    pool_tile(seg, mp)
                    p2q.append(mp)
                    yield
                if fi >= 3 and fi % 2 == 1:
                    kind2, _, i2 = fm[fi - 3]
                    qk_norm2(gi, 0 if kind2 == "q" else 1, i2, fi - 3)
            nf = len(fm)
            yield
            tm(1, 1)
            kind2, _, i2 = fm[nf - 2]
            qk_norm2(gi, 0 if kind2 == "q" else 1, i2, nf - 2)
            yield
            while ptiles or p2q:
                if p2q:
                    pool_tile2(seg, p2q.pop(0))
                    yield
                    continue
                mp = ptiles.pop(0)
                pool_tile(seg, mp)
                p2q.append(mp)
                yield
                continue
                yield

        it_ctr = [0]

        def c_prep(seg, j):
            jp = j % 2
            if any(v[0] == "reg" for v in attn_variants(seg, j)):
                ACT(kmrg[:, jp, :, 0:64], kT[:, :, s8(j + 2), 0:64], AF.Copy, [("kT", hp, s8(j + 2)) for hp in range(4)],
                    [("kmrg", jp)])
                ACT(kmrg[:, jp, :, 64:128], kT[:, :, s8(j - 2), 64:128], AF.Copy, [("kT", hp, s8(j - 2)) for hp in range(4)],
                    [("kmrg", jp)])
                ACT(vmrg[0:64, jp, :], vr[0:64, s8(j + 2), :], AF.Copy, [("vr", s8(j + 2))], [("vmrg", jp)])
                ACT(vmrg[64:128, jp, :], vr[64:128, s8(j - 2), :], AF.Copy, [("vr", s8(j - 2))], [("vmrg", jp)])

        def c_qtiles(gi):
            seg, b = GB[gi]
            return [j for j in (2 * b, 2 * b + 1) if isq(seg, j)]

        def stageC(gi):
            seg, b = GB[gi]
            pend = None
            qs = c_qtiles(gi)
            for j in qs:
                variants = attn_variants(seg, j)
                jp = j % 2
                for hp in range(4):
                    if hp == 2 and j == qs[0] and len(qs) > 1:
                        c_prep(seg, qs[1])
                    for vi, var in enumerate(variants):
                        st = it_ctr[0] % 2
                        it_ctr[0] += 1
                        fl = var[-1]
                        hA, hB = 2 * hp, 2 * hp + 1
                        qk = ("qT", hp, s8(j))
                        if var[0] == "reg":
                            e0 = 0
                            blocks = [(("kT", j + 1), 0, 128, 0), (("kT", j), 0, 128, 128), (("kT", j - 1), 0, 128, 256),
                                      (("kT", j - 2), 0, 64, 384), (("mrg", jp), 64, 128, 448)]
                        else:
                            kb_hi = var[1]
                            e0 = NT_REG + 6 - 2 * (kb_hi - j)
                            blocks = [(("kT", kb_hi - i), 0, 128, 128 * i) for i in range(4)]
                        for (src, ql, qh, sc) in blocks:
                            n = qh - ql
                            for hh in range(2):
                                r0 = 64 * hh
                                if src[0] == "kT":
                                    lhs = kT[r0:r0 + 64, hp, s8(src[1]), :]
                                    rk = ("kT", hp, s8(src[1]))
                                else:
                                    lhs = kmrg[r0:r0 + 64, src[1], hp, :]
                                    rk = ("kmrg", src[1])
                                MM(bank(3 + hh, sc, sc + n), lhs, qT[r0:r0 + 64, hp, s8(j), ql:qh], True, True,
                                   [rk, qk], bk(3 + hh))
                        slots = [2 * st, 2 * st + 1]
                        pk = [("pet", slots[0]), ("pet", slots[1])]
                        tk = [("ptt", slots[0]), ("ptt", slots[1])]
                        ACT(pet[:, 2 * st:2 * st + 2, :].rearrange("p a b -> p (a b)"), pm[:, 3 * 512:5 * 512], AF.Exp,
                            bk(3) + bk(4), pk)
                        TT(ptt[:, 2 * st:2 * st + 2, :], pet[:, 2 * st:2 * st + 2, :],
                           etab[:, hA:hA + 2, e0:e0 + 8, :].rearrange("p h t c -> p h (t c)"), ALU.mult,
                           pk + ETAB, tk)
                        yield
                        if pend is not None:
                            pend()
                            yield

                        def pv(st=st, slots=slots, blocks=blocks, hA=hA, hB=hB, fl=fl, vi=vi, hp=hp, j=j):
                            ndb = 5 + st
                            for grp in range(2):
                                ob = ndb * 512 + grp * 128
                                for ii, (src, ql, qh, sc) in enumerate(blocks):
                                    n = qh - ql
                                    for hh, hd in enumerate([hA, hB]):
                                        po = 64 * hh
                                        if grp == 1:
                                            lhs, rds = ones64[:], ["c_ones"]
                                        elif src[0] == "kT":
                                            lhs, rds = vr[:, s8(src[1]), hd * 64:(hd + 1) * 64], [("vr", s8(src[1]))]
                                        else:
                                            lhs, rds = vmrg[:, src[1], hd * 64:(hd + 1) * 64], [("vmrg", src[1])]
                                        MM(pm[po:po + 64, ob + ql:ob + qh], lhs, ptt[:, slots[hh], sc:sc + n],
                                           ii == 0, ii == len(blocks) - 1, rds + [("ptt", slots[hh])], bk(ndb))
                            num = bank(ndb, 0, 128)
                            den = bank(ndb, 128, 256)
                            ACT(rec[:, st, :], den, AF.Ln, bk(ndb), [("rec", st)])
                            ACT(rec[:, st, :], rec[:, st, :], AF.Exp, [("rec", st)], [("rec", st)], scale=-1.0)
                            if fl is None:
                                TT(tf1[:, st, :], num, rec[:, st, :], ALU.mult, bk(ndb) + [("rec", st)], [("tf1", st)])
                            else:
                                STT(tf1[:, st, :], num, flags[:, fl:fl + 1], rec[:, st, :], ALU.mult, ALU.mult,
                                    bk(ndb) + [("rec", st), "c_flags"], [("tf1", st)])
                            mk_ = ("mx", 4 + hp, s8(j))
                            if vi == 0:
                                TT(mx[:, 4 + hp, s8(j), :], tf1[:, st, :], sg[:, hp, s8(j), :], ALU.mult,
                                   [("tf1", st), ("sg", hp, s8(j))], [mk_])
                            else:
                                TT(tf2[:], tf1[:, st, :], sg[:, hp, s8(j), :], ALU.mult,
                                   [("tf1", st), ("sg", hp, s8(j))], ["tf2"])
                                TT(mx[:, 4 + hp, s8(j), :], mx[:, 4 + hp, s8(j), :], tf2[:], ALU.add, ["tf2", mk_], [mk_])
                        pend = pv
            if pend is not None:
                pend()
                yield

        def stageD(gi):
            seg, b = GB[gi]
            for j in (2 * b, 2 * b + 1):
                if not isq(seg, j):
                    continue
                sl = j % 2
                xrow = (seg["x0"] + j) * 128
                yrow = (seg["y0"] + j - seg["qlo"]) * 128
                xk = "xr%d" % sl
                for half in range(2):
                    rb_ = nb()
                    for c in range(8):
                        MM(bank(rb_), mx[:, c, s8(j), :], wob[:, c, half * 512:(half + 1) * 512], c == 0, c == 7,
                           [("mx", c, s8(j))] + WOB, bk(rb_))
                    TT(xr[:, sl, half * 512:(half + 1) * 512], bank(rb_), xr[:, sl, half * 512:(half + 1) * 512], ALU.add,
                       bk(rb_) + [xk], [xk])
                    yield
                DMA(yout[yrow:yrow + 128, :], xr[:, sl, :], reads=[xk], key="yo%d" % sl, final=True)

        def drain(g):
            for _ in g:
                pass

        def merge(gens):
            gens = [[g, 0, max(u, 1)] for g, u in gens]
            while gens:
                gens.sort(key=lambda t: (t[1] + 1) / t[2])
                t = gens[0]
                try:
                    next(t[0])
                    t[1] += 1
                except StopIteration:
                    gens.remove(t)

        NG = len(GB)
        import os
        NIT = int(os.environ.get("DBG_NIT", NG + 3))
        STG = os.environ.get("DBG_STAGES", "AGBCD")
        stageA0(0)
        drain(stageA(0))
        stageA0(1)
        for k in range(0, min(NG + 3, NIT)):
            if 0 <= k - 3 < NG:
                stageD0(k - 3)
            if k < NG and "G" in STG:
                drain(stageB_gates(k))
            if 0 <= k - 2 < NG and c_qtiles(k - 2):
                c_prep(GB[k - 2][0], c_qtiles(k - 2)[0])
            gl = []
            if k < NG and "B" in STG:
                gl.append((stageB_rest(k), 16))
            if 0 <= k - 2 < NG and "C" in STG:
                gl.append((stageC(k - 2), int(os.environ.get('K_CW', 14))))
            if 0 <= k - 3 < NG and "D" in STG:
                gl.append((stageD(k - 3), 4))
            if k + 1 < NG and "A" in STG:
                gl.append((stageA(k + 1), 4))
            merge(gl)
            if k + 2 < NG:
                stageA0(k + 2)
            if k == 0:
                setup_part2()

        P.emit(block, sems, dsem)
    return nc


_NC_CACHE = {}


def kernel(x_prompt, x_sample, norm_g, w_in, w_pool, pool_scale, q_norm_g, k_norm_g, rpb, w_out):
    x_prompt = np.asarray(x_prompt, np.float32)
    x_sample = np.asarray(x_sample, np.float32)
    if "nc" not in _NC_CACHE:
        _NC_CACHE["nc"] = build_program()
    nc = _NC_CACHE["nc"]
    biasg, maskg = build_bias_mask(np.asarray(rpb, np.float32)[0])
    ident = np.eye(128, dtype=np.float32).astype(bf)
    blk = np.kron(np.eye(2, dtype=np.float32), np.ones((64, 64), np.float32)).astype(bf)
    ones64 = np.ones((128, 64), np.float32).astype(bf)
    ng = np.ascontiguousarray(np.asarray(norm_g, np.float32)[0].reshape(8, 128).T)
    psc = np.ascontiguousarray(np.asarray(pool_scale, np.float32)[0].reshape(4, 128).T)
    gq = np.asarray(q_norm_g, np.float32)[0]
    gk = np.asarray(k_norm_g, np.float32)[0]
    gqk = np.ascontiguousarray(np.stack([np.concatenate([gq, gq]), np.concatenate([gk, gk])], axis=1))
    w_in0 = np.ascontiguousarray(np.asarray(w_in, np.float32)[0])
    w_out0 = np.ascontiguousarray(np.asarray(w_out, np.float32)[0])
    w_pool0 = np.ascontiguousarray(np.asarray(w_pool, np.float32)[0].reshape(512, 128))
    in_maps = []
    for c in range(NCORES):
        sb, sq = c // 4, c % 4
        r0, r1 = sq * 2048 - 256, sq * 2048 + 2048 + 256
        piece = np.zeros((2560, D), np.float32)
        a0, a1 = max(r0, 0), min(r1, 8192)
        piece[a0 - r0:a1 - r0] = x_sample[sb, a0:a1]
        xin = np.concatenate([x_prompt[2 * c], x_prompt[2 * c + 1], piece], axis=0)
        top, bot = (sq == 0), (sq == 3)
        fl = np.tile(np.array([[float(top), 1.0 - float(top), float(bot), 1.0 - float(bot)]], np.float32), (128, 1))
        bands = build_bands(top, bot).reshape(128, NB * 128).astype(bf)
        in_maps.append(dict(xin=np.ascontiguousarray(xin), w_in=w_in0, w_out=w_out0, w_pool=w_pool0, ng=ng, psc=psc,
                            gqk=gqk, biasg=biasg, maskg=maskg.astype(bf), bands=bands, ident=ident, blk=blk, ones64=ones64,
                            flags=fl))
    res = run_bass_kernel_spmd(nc, in_maps, core_ids=list(range(NCORES)))
    y_prompt = np.empty((16, 4096, D), np.float32)
    y_sample = np.empty((2, 8192, D), np.float32)
    for c in range(NCORES):
        y = np.asarray(res.results[c]["yout"])
        y_prompt[2 * c] = y[0:4096]
        y_prompt[2 * c + 1] = y[4096:8192]
        y_sample[c // 4, (c % 4) * 2048:(c % 4 + 1) * 2048] = y[8192:10240]
    return (y_prompt, y_sample)
```
